# Optimizing a Trainium2 kernel written in Bass

```python
import math
import jax
import jax.numpy as jnp
from jax import lax
import numpy as np

D_MODEL = 2048
BATCH = 8
SEQ = 2048
DEPTH = 2

GRID_W = 64
CTX_LEN = 256
MIX_WIDTH = 2 * D_MODEL
GATE_WIDTH = MIX_WIDTH
DIFF_V = 128
DIFF_HEADS = D_MODEL // DIFF_V
DIFF_QK = DIFF_V // 2
MLSTM_V = 256
MLSTM_HEADS = D_MODEL // MLSTM_V
MLSTM_QK = MLSTM_V // 2
RET_HEADS = 8
RET_DK = D_MODEL // RET_HEADS
RET_DV = D_MODEL // RET_HEADS
MLA_HEADS = 16
MLA_NOPE = 128
MLA_ROPE = 64
MLA_V = D_MODEL // MLA_HEADS
MLA_Q_LORA = D_MODEL // 4
MLA_KV_LORA = D_MODEL // 8

CHUNK = 64
Q_BLOCK = 128
ROPE_BASE = 10000.0
NORM_EPS = 1e-5

AB_SPLITS = (DIFF_HEADS * 2 * DIFF_QK, DIFF_HEADS * 2 * DIFF_QK, DIFF_HEADS * DIFF_V,
             MLSTM_HEADS * MLSTM_QK, MLSTM_HEADS * MLSTM_QK, MLSTM_HEADS * MLSTM_V,
             MLSTM_HEADS * MLSTM_V, 2 * MLSTM_HEADS, 2 * MLSTM_HEADS, GATE_WIDTH)
CD_SPLITS = (RET_HEADS * RET_DK, RET_HEADS * RET_DK, RET_HEADS * RET_DV,
             MLA_Q_LORA, MLA_KV_LORA, MLA_ROPE, GATE_WIDTH)

kernel_name = 'hybrid_diffattn_mlstm_retention_mla_dit'

F32 = jnp.float32


def split_cols(a, sizes):
    return jnp.split(a, [int(s) for s in np.cumsum(sizes)[:-1]], axis=-1)


def heads(a, n_heads):
    b, l, _ = a.shape
    return a.reshape(b, l, n_heads, -1).transpose(0, 2, 1, 3)


def merge_heads(a):
    b, h, l, d = a.shape
    return a.transpose(0, 2, 1, 3).reshape(b, l, h * d)


def layer_norm(x, g, b):
    xf = x.astype(F32)
    xc = xf - xf.mean(-1, keepdims=True)
    y = xc * lax.rsqrt((xc * xc).mean(-1, keepdims=True) + NORM_EPS)
    return (y * g + b).astype(x.dtype)


def rms_norm(x, g):
    xf = x.astype(F32)
    y = xf * lax.rsqrt((xf * xf).mean(-1, keepdims=True) + NORM_EPS)
    return (y * g).astype(x.dtype)


def head_layer_norm(a, g):
    af = a.astype(F32)
    ac = af - af.mean(-1, keepdims=True)
    y = ac * lax.rsqrt((ac * ac).mean(-1, keepdims=True) + NORM_EPS)
    return (merge_heads(y) * g).astype(g.dtype)


def head_rms_norm(a, g):
    af = a.astype(F32)
    y = af * lax.rsqrt((af * af).mean(-1, keepdims=True) + NORM_EPS)
    return (merge_heads(y) * g).astype(g.dtype)


def axial_rope(n_tokens, rot_dim):
    n_rows = n_tokens // GRID_W
    t = jnp.arange(n_tokens)
    row = jnp.repeat(jnp.arange(n_rows), GRID_W).astype(F32)
    col = (t % GRID_W).astype(F32)
    n_freq = rot_dim // 4
    inv_freq = ROPE_BASE ** (-jnp.arange(n_freq, dtype=F32) / n_freq)
    ang = jnp.concatenate([row[:, None] * inv_freq, col[:, None] * inv_freq], axis=-1)
    return jnp.cos(ang), jnp.sin(ang)


def apply_rope(x, rope):
    cos, sin = (t.astype(x.dtype) for t in rope)
    x1, x2 = jnp.split(x, 2, axis=-1)
    return jnp.concatenate([x1 * cos - x2 * sin, x1 * sin + x2 * cos], axis=-1)


def over_query_blocks(fn, *qs):
    b, h, l, _ = qs[0].shape
    nb = l // Q_BLOCK
    blocks = tuple(jnp.moveaxis(q.reshape(b, h, nb, Q_BLOCK, q.shape[-1]), 2, 0) for q in qs)
    out = lax.map(lambda bl: fn(*bl), blocks)
    return jnp.moveaxis(out, 0, 2).reshape(b, h, l, out.shape[-1])


def softmax_attention(q, k, v, scale):
    def block(qb):
        s = jnp.einsum('bhqd,bhkd->bhqk', qb, k).astype(F32) * scale
        return jnp.einsum('bhqk,bhkd->bhqd', jax.nn.softmax(s, axis=-1).astype(v.dtype), v)
    return over_query_blocks(block, q)


def diff_attention(q1, q2, k1, k2, v, lam):
    scale = q1.shape[-1] ** -0.5

    def block(a1, a2):
        s1 = jnp.einsum('bhqd,bhkd->bhqk', a1, k1).astype(F32) * scale
        s2 = jnp.einsum('bhqd,bhkd->bhqk', a2, k2).astype(F32) * scale
        p = jax.nn.softmax(s1, axis=-1) - lam * jax.nn.softmax(s2, axis=-1)
        return jnp.einsum('bhqk,bhkd->bhqd', p.astype(v.dtype), v)
    return over_query_blocks(block, q1, q2)


def to_chunks(a):
    b, h, l = a.shape[:3]
    return jnp.moveaxis(a.reshape((b, h, l // CHUNK, CHUNK) + a.shape[3:]), 2, 0)


def from_chunks(a):
    a = jnp.moveaxis(a, 0, 2)
    return a.reshape(a.shape[:2] + (a.shape[2] * a.shape[3],) + a.shape[4:])


def mlstm_scan(seqs, state):
    lower = jnp.tril(jnp.ones((CHUNK, CHUNK), dtype=bool))

    def step(carry, xs):
        c_st, n_st, m_st = carry
        qc, kc, vc, ic, fc = xs
        b = jnp.cumsum(fc, axis=-1)
        d = jnp.where(lower, b[..., :, None] - b[..., None, :] + ic[..., None, :], -jnp.inf)
        inter = b + m_st[..., None]
        m_t = jnp.maximum(d.max(-1), inter)
        w = jnp.exp(d - m_t[..., None])
        a = jnp.exp(inter - m_t)
        s = jnp.einsum('bhtd,bhsd->bhts', qc, kc) * w
        num = a[..., None] * jnp.einsum('bhtd,bhdv->bhtv', qc, c_st) + jnp.einsum('bhts,bhsv->bhtv', s, vc)
        den = a * jnp.einsum('bhtd,bhd->bht', qc, n_st) + s.sum(-1)
        h = num / jnp.maximum(jnp.abs(den), jnp.exp(-m_t))[..., None]
        b_end = b[..., -1]
        g = b_end[..., None] - b + ic
        m_new = jnp.maximum(b_end + m_st, g.max(-1))
        decay = jnp.exp(b_end + m_st - m_new)
        wk = jnp.exp(g - m_new[..., None])
        c_st = decay[..., None, None] * c_st + jnp.einsum('bhs,bhsd,bhsv->bhdv', wk, kc, vc)
        n_st = decay[..., None] * n_st + jnp.einsum('bhs,bhsd->bhd', wk, kc)
        return (c_st, n_st, m_new), h

    state, hs = lax.scan(step, state, tuple(to_chunks(a) for a in seqs))
    return from_chunks(hs), state


def retention_scan(seqs, log_gamma, state):
    q_seq = seqs
    pos = jnp.arange(CHUNK, dtype=F32)
    rel = pos[:, None] - pos[None, :]
    decay = jnp.where(rel >= 0, jnp.exp(log_gamma[:, None, None] * jnp.maximum(rel, 0.0)), 0.0)
    q_decay = jnp.exp(log_gamma[:, None] * (pos + 1.0))[..., None]
    k_decay = jnp.exp(log_gamma[:, None] * (CHUNK - 1.0 - pos))
    chunk_decay = jnp.exp(log_gamma * CHUNK)[:, None, None]

    def step(s_st, xs):
        qc, kc, vc = xs
        scores = jnp.einsum('bhtd,bhsd->bhts', qc, kc) * decay
        out = jnp.einsum('bhts,bhsv->bhtv', scores, vc) + q_decay * jnp.einsum('bhtd,bhdv->bhtv', qc, s_st)
        s_st = chunk_decay * s_st + jnp.einsum('hs,bhsd,bhsv->bhdv', k_decay, kc, vc)
        return s_st, out

    state, outs = lax.scan(step, state, tuple(to_chunks(a) for a in q_seq))
    return from_chunks(outs), state


def bidirectional_scan(scan_fwd, scan_bwd, lat_f, lat_b, ctx_f, ctx_b, state0):
    flip = lambda seqs: tuple(jnp.flip(a, axis=2) for a in seqs)
    hc_f, st_f = scan_fwd(ctx_f, state0)
    hl_f, _ = scan_fwd(lat_f, st_f)
    hc_b, st_b = scan_bwd(flip(ctx_b), state0)
    hl_b, _ = scan_bwd(flip(lat_b), st_b)
    return hl_f + jnp.flip(hl_b, axis=2), hc_f + jnp.flip(hc_b, axis=2)


def diff_mlstm_sublayer(h, hc, w_in, b_if, lam_vec, diff_g, mlstm_g, w_out, lam_init, rope, need_ctx):
    lat = split_cols(h @ w_in, AB_SPLITS)
    cx = split_cols(hc @ w_in, AB_SPLITS)

    lv = lam_vec.astype(F32)
    lam = jnp.exp(jnp.sum(lv[0] * lv[1])) - jnp.exp(jnp.sum(lv[2] * lv[3])) + lam_init

    def diff_qkv(parts, rope_t):
        q1, q2 = jnp.split(heads(parts[0], DIFF_HEADS), 2, axis=-1)
        k1, k2 = jnp.split(heads(parts[1], DIFF_HEADS), 2, axis=-1)
        if rope_t is not None:
            q1, q2, k1, k2 = (apply_rope(t, rope_t) for t in (q1, q2, k1, k2))
        return q1, q2, k1, k2, heads(parts[2], DIFF_HEADS)

    q1, q2, k1, k2, v = diff_qkv(lat, rope)
    cq1, cq2, ck1, ck2, cv = diff_qkv(cx, None)
    cat = lambda a, b: jnp.concatenate([a, b], axis=2)
    a_lat = diff_attention(q1, q2, cat(ck1, k1), cat(ck2, k2), cat(cv, v), lam)

    def mlstm_seqs(parts):
        bt, lt, _ = parts[0].shape
        q = heads(parts[3], MLSTM_HEADS).astype(F32) * MLSTM_QK ** -0.5
        k = heads(parts[4], MLSTM_HEADS).astype(F32)
        vv = heads(parts[5], MLSTM_HEADS).astype(F32)
        gi = (parts[7].reshape(bt, lt, 2, MLSTM_HEADS) + b_if[:2]).astype(F32).transpose(2, 0, 3, 1)
        gf = jax.nn.log_sigmoid((parts[8].reshape(bt, lt, 2, MLSTM_HEADS) + b_if[2:]).astype(F32)).transpose(2, 0, 3, 1)
        return (q, k, vv, gi[0], gf[0]), (q, k, vv, gi[1], gf[1])

    lat_f, lat_b = mlstm_seqs(lat)
    ctx_f, ctx_b = mlstm_seqs(cx)
    bsz = h.shape[0]
    state0 = (jnp.zeros((bsz, MLSTM_HEADS, MLSTM_QK, MLSTM_V), F32),
              jnp.zeros((bsz, MLSTM_HEADS, MLSTM_QK), F32),
              jnp.zeros((bsz, MLSTM_HEADS), F32))
    m_lat, m_ctx = bidirectional_scan(mlstm_scan, mlstm_scan, lat_f, lat_b, ctx_f, ctx_b, state0)

    def combine(a_out, m_out, parts):
        a_n = head_rms_norm(a_out, diff_g) * (1.0 - lam_init)
        m_n = head_layer_norm(m_out, mlstm_g) * jax.nn.sigmoid(parts[6])
        return (jnp.concatenate([a_n, m_n], axis=-1) * jax.nn.silu(parts[9])) @ w_out

    y = combine(a_lat, m_lat, lat)
    yc = combine(diff_attention(cq1, cq2, ck1, ck2, cv, lam), m_ctx, cx) if need_ctx else None
    return y, yc


def retention_mla_sublayer(h, hc, w_in, ret_decay, ret_g, q_norm_g, w_uq, kv_norm_g, w_ukv, w_out,
                           rope_ret, rope_mla, need_ctx):
    lat = split_cols(h @ w_in, CD_SPLITS)
    cx = split_cols(hc @ w_in, CD_SPLITS)

    def ret_seqs(parts, rope_t):
        q = heads(parts[0], RET_HEADS)
        k = heads(parts[1], RET_HEADS)
        if rope_t is not None:
            q, k = apply_rope(q, rope_t), apply_rope(k, rope_t)
        return (q.astype(F32), k.astype(F32) * RET_DK ** -0.5, heads(parts[2], RET_HEADS).astype(F32))

    lg = jax.nn.log_sigmoid(ret_decay.astype(F32))
    lat_s = ret_seqs(lat, rope_ret)
    ctx_s = ret_seqs(cx, None)
    state0 = jnp.zeros((h.shape[0], RET_HEADS, RET_DK, RET_DV), F32)
    r_lat, r_ctx = bidirectional_scan(lambda s, st: retention_scan(s, lg[0], st),
                                      lambda s, st: retention_scan(s, lg[1], st),
                                      lat_s, lat_s, ctx_s, ctx_s, state0)

    def mla_qkv(parts, rope_t):
        q_nope, q_rope = jnp.split(heads(rms_norm(parts[3], q_norm_g) @ w_uq, MLA_HEADS), [MLA_NOPE], axis=-1)
        k_nope, vv = jnp.split(heads(rms_norm(parts[4], kv_norm_g) @ w_ukv, MLA_HEADS), [MLA_NOPE], axis=-1)
        k_rope = parts[5][:, None]
        if rope_t is not None:
            q_rope, k_rope = apply_rope(q_rope, rope_t), apply_rope(k_rope, rope_t)
        k_rope = jnp.broadcast_to(k_rope, k_nope.shape[:3] + (MLA_ROPE,))
        return jnp.concatenate([q_nope, q_rope], axis=-1), jnp.concatenate([k_nope, k_rope], axis=-1), vv

    q, k, v = mla_qkv(lat, rope_mla)
    cq, ck, cv = mla_qkv(cx, None)
    scale = (MLA_NOPE + MLA_ROPE) ** -0.5
    d_lat = softmax_attention(q, jnp.concatenate([ck, k], axis=2), jnp.concatenate([cv, v], axis=2), scale)

    def combine(r_out, d_out, parts):
        mixed = jnp.concatenate([head_layer_norm(r_out, ret_g), merge_heads(d_out)], axis=-1)
        return (mixed * jax.nn.silu(parts[6])) @ w_out

    y = combine(r_lat, d_lat, lat)
    yc = combine(r_ctx, softmax_attention(cq, ck, cv, scale), cx) if need_ctx else None
    return y, yc


def setup_inputs(seed: int = 0) -> dict:
    key = jax.random.key(seed)
    ks = jax.random.split(key, 22)
    n_even = (DEPTH + 1) // 2
    n_odd = DEPTH // 2
    beta = (8.0 * DEPTH) ** -0.25

    def nrm(k, shape, s):
        return jax.random.normal(k, shape, F32) * s

    def gain(k, shape):
        return 1.0 + nrm(k, shape, 0.01)

    ab_in = int(sum(AB_SPLITS))
    cd_in = int(sum(CD_SPLITS))
    f_bias = jnp.linspace(3.0, 6.0, MLSTM_HEADS, dtype=F32)
    i_bias = jnp.full((MLSTM_HEADS,), -2.0, F32)
    if_base = jnp.stack([i_bias, i_bias, f_bias, f_bias])
    gamma = 1.0 - 2.0 ** (-5.0 - np.arange(RET_HEADS, dtype=np.float32))
    ret_logit = jnp.asarray(np.log(gamma / (1.0 - gamma)), F32)
    return {
        'x': nrm(ks[0], (BATCH, SEQ, D_MODEL), 1.0),
        'c': nrm(ks[1], (BATCH, D_MODEL), 1.0),
        'ctx': nrm(ks[2], (BATCH, CTX_LEN, D_MODEL), 1.0),
        'c_ctx': nrm(ks[3], (D_MODEL,), 1.0),
        'ada_w': nrm(ks[4], (DEPTH, D_MODEL, 3 * D_MODEL), D_MODEL ** -0.5),
        'ada_b': nrm(ks[5], (DEPTH, 3 * D_MODEL), 0.01),
        'ln_g': gain(ks[6], (DEPTH, D_MODEL)),
        'ln_b': nrm(ks[7], (DEPTH, D_MODEL), 0.01),
        'ab_w_in': nrm(ks[8], (n_even, D_MODEL, ab_in), D_MODEL ** -0.5),
        'ab_b_if': if_base + nrm(ks[9], (n_even, 4, MLSTM_HEADS), 0.1),
        'diff_lam': nrm(ks[10], (n_even, 4, DIFF_QK), 0.1),
        'diff_norm_g': gain(ks[11], (n_even, DIFF_HEADS * DIFF_V)),
        'mlstm_norm_g': gain(ks[12], (n_even, MLSTM_HEADS * MLSTM_V)),
        'ab_w_out': nrm(ks[13], (n_even, MIX_WIDTH, D_MODEL), beta * MIX_WIDTH ** -0.5),
        'cd_w_in': nrm(ks[14], (n_odd, D_MODEL, cd_in), D_MODEL ** -0.5),
        'ret_decay': ret_logit + nrm(ks[15], (n_odd, 2, RET_HEADS), 0.01),
        'ret_norm_g': gain(ks[16], (n_odd, RET_HEADS * RET_DV)),
        'mla_q_norm_g': gain(ks[17], (n_odd, MLA_Q_LORA)),
        'mla_w_uq': nrm(ks[18], (n_odd, MLA_Q_LORA, MLA_HEADS * (MLA_NOPE + MLA_ROPE)), MLA_Q_LORA ** -0.5),
        'mla_kv_norm_g': gain(ks[19], (n_odd, MLA_KV_LORA)),
        'mla_w_ukv': nrm(ks[20], (n_odd, MLA_KV_LORA, MLA_HEADS * (MLA_NOPE + MLA_V)), MLA_KV_LORA ** -0.5),
        'cd_w_out': nrm(ks[21], (n_odd, MIX_WIDTH, D_MODEL), beta * MIX_WIDTH ** -0.5),
    }


def reference(x, c, ctx, c_ctx, ada_w, ada_b, ln_g, ln_b,
              ab_w_in, ab_b_if, diff_lam, diff_norm_g, mlstm_norm_g, ab_w_out,
              cd_w_in, ret_decay, ret_norm_g, mla_q_norm_g, mla_w_uq, mla_kv_norm_g, mla_w_ukv, cd_w_out):
    alpha = (2.0 * DEPTH) ** 0.25
    n_lat = x.shape[1]
    rope_diff = axial_rope(n_lat, DIFF_QK)
    rope_ret = axial_rope(n_lat, RET_DK)
    rope_mla = axial_rope(n_lat, MLA_ROPE)
    sc = jax.nn.silu(c)
    scc = jax.nn.silu(c_ctx)
    xc = ctx
    for layer in range(DEPTH):
        need_ctx = layer < DEPTH - 1
        shift, scale, gate = jnp.split(sc @ ada_w[layer] + ada_b[layer], 3, axis=-1)
        shift_c, scale_c, gate_c = jnp.split(scc @ ada_w[layer] + ada_b[layer], 3, axis=-1)
        h = x * (1.0 + scale[:, None]) + shift[:, None]
        hc = xc * (1.0 + scale_c) + shift_c
        i = layer // 2
        if layer % 2 == 0:
            lam_init = 0.8 - 0.6 * math.exp(-0.3 * layer)
            y, yc = diff_mlstm_sublayer(h, hc, ab_w_in[i], ab_b_if[i], diff_lam[i], diff_norm_g[i],
                                        mlstm_norm_g[i], ab_w_out[i], lam_init, rope_diff, need_ctx)
        else:
            y, yc = retention_mla_sublayer(h, hc, cd_w_in[i], ret_decay[i], ret_norm_g[i], mla_q_norm_g[i],
                                           mla_w_uq[i], mla_kv_norm_g[i], mla_w_ukv[i], cd_w_out[i],
                                           rope_ret, rope_mla, need_ctx)
        x = layer_norm(alpha * x + gate[:, None] * y, ln_g[layer], ln_b[layer])
        if need_ctx:
            xc = layer_norm(alpha * xc + gate_c * yc, ln_g[layer], ln_b[layer])
    return x
```

```python
import math
from contextlib import ExitStack

import numpy as np
import concourse.bass as bass
import concourse.mybir as mybir
from concourse.bass_utils import run_bass_kernel_spmd

F32 = mybir.dt.float32
BF16 = mybir.dt.bfloat16
AF = mybir.ActivationFunctionType
ALU = mybir.AluOpType
AX = mybir.AxisListType

D = 2048
SEQ = 2048
CTX = 256
NT = SEQ + CTX
NTB = NT // 128
KC = D // 128
AB_IN = 16416
CD_IN = 11072
EPS = 1e-5
ALPHA = (2.0 * 2) ** 0.25
GRID_W = 64

EPOCH = 12000
NDMASEM = 24
ENGS = ("pe", "act", "dve", "pool", "sp")


class Op:
    __slots__ = ("eng", "fn", "reads", "writes", "dma", "deps", "signal", "sigidx", "dmasem", "dmaval")

    def __init__(self, eng, fn, reads, writes, dma):
        self.eng = eng
        self.fn = fn
        self.reads = reads
        self.writes = writes
        self.dma = dma
        self.deps = ()
        self.signal = False
        self.sigidx = 0
        self.dmasem = -1
        self.dmaval = 0


class Sched:
    def __init__(self, nc, es):
        self.nc = nc
        self.ops = []
        self.cnt = {e: 0 for e in ENGS}
        self.sems = {e: [] for e in ENGS}
        self.es = es
        self.dmasems = [es.enter_context(nc.semaphore(f"dq{i}")) for i in range(NDMASEM)]
        self.dma_n = 0
        self.dma_m = 0
        self.dma_last = [None] * NDMASEM
        self.dma_cur = [0] * NDMASEM
        self.seen = {e: {f: 0 for f in ENGS} for e in ENGS}
        self.seen_dma = {e: [0] * NDMASEM for e in ENGS}
        self.nops = 0

    def sem_for(self, eng, sigidx):
        ep, v = divmod(sigidx - 1, EPOCH)
        lst = self.sems[eng]
        while len(lst) <= ep:
            lst.append(self.es.enter_context(self.nc.semaphore(f"s_{eng}_{len(lst)}")))
        return lst[ep], v + 1

    def add(self, eng, fn, reads=(), writes=(), dma=False):
        reads = tuple(reads)
        writes = tuple(writes)
        px = tuple(k for k in reads if isinstance(k, str) and k[:1] == "p" and k not in writes)
        if px:
            writes = writes + px
        self.ops.append(Op(eng, fn, reads, writes, dma))

    def pe(self, fn, reads=(), writes=()):
        self.add("pe", fn, reads, writes)

    def act(self, fn, reads=(), writes=()):
        self.add("act", fn, reads, writes)

    def dve(self, fn, reads=(), writes=()):
        self.add("dve", fn, reads, writes)

    def pool(self, fn, reads=(), writes=()):
        self.add("pool", fn, reads, writes)

    def dma(self, out, in_, reads=(), writes=(), q="sp", **kw):
        self.add(q, lambda e: e.dma_start(out=out, in_=in_, **kw), reads, writes, dma=True)

    def flush(self, name=None):
        ops = self.ops
        self.ops = []
        n = len(ops)
        self.nops += n
        last_writer = {}
        rd_eng = {}
        rd_dma = {}
        for i, op in enumerate(ops):
            deps = set()
            for r in op.reads:
                j = last_writer.get(r)
                if j is not None:
                    deps.add(j)
            for w in op.writes:
                j = last_writer.get(w)
                if j is not None:
                    deps.add(j)
                d = rd_eng.get(w)
                if d:
                    deps.update(d.values())
                d = rd_dma.get(w)
                if d:
                    deps.update(d)
            if op.dma:
                if op.eng == "sp":
                    s = self.dma_n % (NDMASEM - 8)
                    self.dma_n += 1
                else:
                    s = NDMASEM - 8 + self.dma_m % 8
                    self.dma_m += 1
                op.dmasem = s
                prev = self.dma_last[s]
                if prev is not None and prev[0] is ops:
                    deps.add(prev[1])
                self.dma_cur[s] += 16
                op.dmaval = self.dma_cur[s]
                self.dma_last[s] = (ops, i)
            deps.discard(i)
            fin = []
            for j in deps:
                dj = ops[j]
                if (not dj.dma) and dj.eng == "pe" and op.eng == "pe" and not op.dma:
                    continue
                if not dj.dma:
                    dj.signal = True
                fin.append(j)
            op.deps = sorted(fin)
            for r in op.reads:
                if op.dma:
                    rd_dma.setdefault(r, []).append(i)
                else:
                    rd_eng.setdefault(r, {})[op.eng] = i
            for w in op.writes:
                last_writer[w] = i
                rd_eng[w] = {}
                rd_dma[w] = []
        lastop = {}
        for i, op in enumerate(ops):
            if not op.dma:
                lastop[op.eng] = i
        for e, i in lastop.items():
            ops[i].signal = True
        for op in ops:
            if op.signal and not op.dma:
                self.cnt[op.eng] += 1
                op.sigidx = self.cnt[op.eng]
        end_cnt = dict(self.cnt)
        end_dma = list(self.dma_cur)
        per_eng = {e: [] for e in ENGS}
        for op in ops:
            per_eng[op.eng].append(op)
        sched = self

        def run(eng_name, E):
            seen = sched.seen[eng_name]
            seen_dma = sched.seen_dma[eng_name]
            for op in per_eng[eng_name]:
                for j in op.deps:
                    dj = ops[j]
                    if dj.dma:
                        if seen_dma[dj.dmasem] >= dj.dmaval:
                            continue
                        E.wait_ge(sched.dmasems[dj.dmasem], dj.dmaval)
                        seen_dma[dj.dmasem] = dj.dmaval
                    else:
                        if seen[dj.eng] >= dj.sigidx:
                            continue
                        s, v = sched.sem_for(dj.eng, dj.sigidx)
                        E.wait_ge(s, v)
                        seen[dj.eng] = dj.sigidx
                ins = op.fn(E)
                if op.dma:
                    ins.then_inc(sched.dmasems[op.dmasem], 16)
                elif op.signal:
                    s, v = sched.sem_for(op.eng, op.sigidx)
                    ins.then_inc(s, 1)
            for f in ENGS:
                if f == eng_name:
                    continue
                if end_cnt[f] > seen[f]:
                    s, v = sched.sem_for(f, end_cnt[f])
                    E.wait_ge(s, v)
                    seen[f] = end_cnt[f]
            for si in range(NDMASEM):
                if end_dma[si] > seen_dma[si]:
                    E.wait_ge(sched.dmasems[si], end_dma[si])
                    seen_dma[si] = end_dma[si]
            seen[eng_name] = end_cnt[eng_name]

        for e in ENGS:
            if end_cnt[e] > 0:
                self.sem_for(e, end_cnt[e])
        with self.nc.Block(name) as block:
            @block.tensor
            def _(E):
                run("pe", E)

            @block.scalar
            def _(E):
                run("act", E)

            @block.vector
            def _(E):
                run("dve", E)

            @block.gpsimd
            def _(E):
                run("pool", E)

            @block.sync
            def _(E):
                run("sp", E)
        self.dma_last = [None] * NDMASEM


def _rope_tables():
    t = np.arange(SEQ)
    row = (t // GRID_W).astype(np.float32)
    col = (t % GRID_W).astype(np.float32)

    def ang(rot_dim):
        nf = rot_dim // 4
        inv = (10000.0 ** (-np.arange(nf, dtype=np.float32) / nf)).astype(np.float32)
        return np.concatenate([row[:, None] * inv, col[:, None] * inv], axis=-1).astype(np.float32)

    a64 = ang(64)
    a256 = ang(256)
    c64 = np.cos(a64).T.astype(np.float32)
    s64 = np.sin(a64).T.astype(np.float32)
    cos64 = np.concatenate([c64, c64, c64, c64], axis=0)
    sin64 = np.concatenate([-s64, s64, -s64, s64], axis=0)
    cos256 = np.cos(a256).T.astype(np.float32)
    sin256 = np.sin(a256).T.astype(np.float32)
    return np.ascontiguousarray(np.stack([cos64, sin64, cos256, sin256], axis=0))


def _const_mats():
    ident = np.eye(128, dtype=np.float32)
    s = np.arange(128)
    mu = (s[:, None] <= s[None, :]).astype(np.float32)
    ml = (s[:, None] >= s[None, :]).astype(np.float32)
    ones = np.ones((128, 128), np.float32)
    return np.ascontiguousarray(np.stack([ident, mu, ml, ones], axis=0))


class Ctx:
    pass


def _evac(S, i, out, in_, reads, writes):
    if i % 2 == 0:
        S.dve(lambda e: e.tensor_copy(out=out, in_=in_), reads, writes)
    else:
        S.act(lambda e: e.copy(out=out, in_=in_), reads, writes)


def phase_adaln(g, S, layer):
    nc = g.nc
    with ExitStack() as es:
        sb = lambda name, shape, dt=F32: es.enter_context(nc.sbuf_tensor(f"L{layer}{name}", shape, dt))
        ct = sb("ad_ct", [128, KC, 2])
        sct = sb("ad_sct", [128, KC, 2])
        rep = [sb(f"ad_rep{w}", [128, KC, 128]) for w in range(2)]
        wt = [sb(f"ad_wt{i}", [128, KC, 512]) for i in range(2)]
        bcol = sb("ad_bcol", [128, 32])
        brow = sb("ad_brow", [128, 2048])
        grow = [sb(f"ad_grow{i}", [128, 512]) for i in range(2)]
        ps = [es.enter_context(nc.psum_tensor(f"L{layer}ad_ps{i}", [128, 512], F32)) for i in range(2)]
        psc = [es.enter_context(nc.psum_tensor(f"L{layer}ad_psc{i}", [128, 8], F32)) for i in range(2)]
        modcol = g.modcol[layer]
        S.dma(ct[:], g.cc[:], writes=["ct"])
        S.act(lambda e: e.activation(out=sct[:], in_=ct[:], func=AF.Silu), ["ct"], ["sct"])
        S.dma(bcol[:], g.ada_bcol[layer], writes=["bcol"])
        S.dma(brow[:], g.ada_b[layer:layer + 1, 4096:6144].partition_broadcast(128), writes=["brow"])
        for w in range(2):
            for kc in range(KC):
                S.dve(lambda e, w=w, kc=kc: e.tensor_scalar(out=rep[w][:, kc, :], in0=g.ones_f[:], scalar1=sct[:, kc, w:w + 1],
                                                             scalar2=None, op0=ALU.mult), ["sct"], [f"rep{w}"])
        aw = g.ada_w[layer].rearrange("(k p) n -> p k n", p=128)
        for t in range(12):
            b = t % 2
            S.dma(wt[b][:], aw[:, :, t * 512:(t + 1) * 512], writes=[f"wt{b}"])
            if t < 8:
                for jj in range(4):
                    j = t * 4 + jj
                    for kc in range(KC):
                        S.pe(lambda e, b=b, jj=jj, kc=kc: e.matmul(psc[b][:, jj * 2:jj * 2 + 2], lhsT=wt[b][:, kc, jj * 128:(jj + 1) * 128],
                                                                  rhs=sct[:, kc, :], start=(kc == 0), stop=(kc == KC - 1)),
                             [f"wt{b}", "sct"], [f"psc{b}"])
                for jj in range(4):
                    j = t * 4 + jj
                    S.dve(lambda e, b=b, jj=jj, j=j: e.tensor_scalar(out=modcol[:, j, :], in0=psc[b][:, jj * 2:jj * 2 + 2],
                                                                      scalar1=bcol[:, j:j + 1], scalar2=(1.0 if j >= 16 else 0.0),
                                                                      op0=ALU.add, op1=ALU.add),
                          [f"psc{b}", "bcol"], [f"psc{b}", "modcol"])
            else:
                c0 = (t - 8) * 512
                for w in range(2):
                    pb = w
                    for kc in range(KC):
                        S.pe(lambda e, b=b, w=w, kc=kc, pb=pb: e.matmul(ps[pb][:], lhsT=rep[w][:, kc, :], rhs=wt[b][:, kc, :],
                                                                       start=(kc == 0), stop=(kc == KC - 1)),
                             [f"wt{b}", f"rep{w}"], [f"ps{pb}"])
                    S.dve(lambda e, w=w, pb=pb, c0=c0: e.tensor_tensor(out=grow[w][:], in0=ps[pb][:], in1=brow[:, c0:c0 + 512], op=ALU.add),
                          [f"ps{pb}", "brow"], [f"ps{pb}", f"grow{w}"])
                    S.dma(g.gate_rows[layer, w:w + 1, c0:c0 + 512], grow[w][0:1, :], reads=[f"grow{w}"])
        S.flush(f"adaln{layer}")


def phase_build_hT(g, S, layer, hT):
    nc = g.nc
    with ExitStack() as es:
        xt = [es.enter_context(nc.sbuf_tensor(f"L{layer}hx{i}", [128, D], F32)) for i in range(2)]
        ps = [es.enter_context(nc.psum_tensor(f"L{layer}hps{i}", [128, 512], F32)) for i in range(4)]
        modcol = g.modcol[layer]
        for tb in range(NTB):
            b = tb % 2
            w = 1 if tb < 2 else 0
            S.dma(xt[b][:], g.xsrc[layer][tb * 128:(tb + 1) * 128, :], writes=[f"xt{b}"])
            for q in range(4):
                pb = (tb * 4 + q) % 4
                for i in range(4):
                    kc = q * 4 + i
                    S.pe(lambda e, b=b, kc=kc, pb=pb, i=i: e.transpose(ps[pb][:, i * 128:(i + 1) * 128], xt[b][:, kc * 128:(kc + 1) * 128], g.ident_f[:]),
                         [f"xt{b}"], [f"ps{pb}"])
                for i in range(4):
                    kc = q * 4 + i
                    S.dve(lambda e, kc=kc, pb=pb, i=i, tb=tb, w=w: e.tensor_scalar(
                        out=hT[:, kc, tb * 128:(tb + 1) * 128], in0=ps[pb][:, i * 128:(i + 1) * 128],
                        scalar1=modcol[:, 16 + kc, w:w + 1], scalar2=modcol[:, kc, w:w + 1], op0=ALU.mult, op1=ALU.add),
                        [f"ps{pb}"], [])
        S.flush(f"hT{layer}")


def linear_phase(g, S, name, XT, nkc, ntok, W, tok_ranges, feat_ranges, ptok, pfeat, wq="pool"):
    nc = g.nc
    ntb = ntok // 128
    with ExitStack() as es:
        wt = [es.enter_context(nc.sbuf_tensor(f"{name}_wt{i}", [128, nkc, 512], BF16)) for i in range(2)]
        hk = max(1, nkc // 2)
        wf = [es.enter_context(nc.sbuf_tensor(f"{name}_wf{i}", [128, hk, 512], F32)) for i in range(2)]
        wfi = [0]

        def load_w(b, c0, n):
            for k0 in range(0, nkc, hk):
                fb = wfi[0] % 2
                wfi[0] += 1
                S.dma(wf[fb][:, :, 0:n], Wv[:, k0:k0 + hk, c0:c0 + n], writes=[f"wf{fb}"])
                S.pool(lambda e, fb=fb, b=b, k0=k0, n=n: e.tensor_copy(out=wt[b][:, k0:k0 + hk, 0:n], in_=wf[fb][:, :, 0:n]),
                       [f"wf{fb}"], [(f"wt{b}", k0)])
        wkeys = lambda b: [(f"wt{b}", k0) for k0 in range(0, nkc, hk)]
        stg = [es.enter_context(nc.sbuf_tensor(f"{name}_stg{i}", [128, 9, 512], F32)) for i in range(2)]
        fst = [es.enter_context(nc.sbuf_tensor(f"{name}_fst{i}", [128, ntok], F32)) for i in range(2)]
        ps = [es.enter_context(nc.psum_tensor(f"{name}_ps{i}", [128, 512], F32)) for i in range(4)]
        Wv = W.rearrange("(k p) n -> p k n", p=128)
        groups = []
        for (c0, c1) in tok_ranges:
            c = c0
            while c < c1:
                n = min(512, c1 - c)
                groups.append(("tok", c, n, 0))
                c += n
        for (c0, c1, r0) in feat_ranges:
            c = c0
            while c < c1:
                n = min(512, c1 - c)
                groups.append(("feat", c, n, r0 + (c - c0)))
                c += n
        ev = 0
        pi = 0
        si = 0
        fi = 0
        for gi, (kind, c0, n, r0) in enumerate(groups):
            b = gi % 2
            load_w(b, c0, n)
            if kind == "tok":
                for tb in range(ntb):
                    half = tb // 9
                    if tb % 9 == 0:
                        sb_i = si % 2
                        si += 1
                    pb = pi % 4
                    pi += 1
                    for kc in range(nkc):
                        S.pe(lambda e, b=b, kc=kc, pb=pb, tb=tb, n=n: e.matmul(ps[pb][:, 0:n], lhsT=XT[:, kc, tb * 128:(tb + 1) * 128],
                                                                            rhs=wt[b][:, kc, 0:n], start=(kc == 0), stop=(kc == nkc - 1)),
                             wkeys(b), [f"ps{pb}"])
                    _evac(S, ev, stg[sb_i][:, tb % 9, 0:n], ps[pb][:, 0:n], [f"ps{pb}"], [(f"stg{sb_i}", tb % 9)])
                    ev += 1
                    if tb % 9 == 8 or tb == ntb - 1:
                        nb = tb % 9 + 1
                        t0 = (tb - nb + 1) * 128
                        dst = ptok[t0:t0 + nb * 128, c0:c0 + n].rearrange("(t p) n -> p t n", p=128)
                        S.dma(dst, stg[sb_i][:, 0:nb, 0:n], reads=[(f"stg{sb_i}", k) for k in range(nb)])
            else:
                for f in range(0, n, 128):
                    m = min(128, n - f)
                    fb = fi % 2
                    fi += 1
                    t0 = 0
                    while t0 < ntok:
                        tn = min(512, ntok - t0)
                        pb = pi % 4
                        pi += 1
                        for kc in range(nkc):
                            S.pe(lambda e, b=b, kc=kc, pb=pb, f=f, m=m, t0=t0, tn=tn: e.matmul(
                                ps[pb][0:m, 0:tn], lhsT=wt[b][:, kc, f:f + m], rhs=XT[:, kc, t0:t0 + tn],
                                start=(kc == 0), stop=(kc == nkc - 1)),
                                wkeys(b), [f"ps{pb}"])
                        _evac(S, ev, fst[fb][0:m, t0:t0 + tn], ps[pb][0:m, 0:tn], [f"ps{pb}"], [(f"fst{fb}", t0)])
                        ev += 1
                        t0 += tn
                    S.dma(pfeat[r0 + f:r0 + f + m, :], fst[fb][0:m, :], reads=[(f"fst{fb}", k) for k in range(0, ntok, 512)])
        S.flush(name)


MINI = None
MINI_HEADS = None
MINI_TBS = None
L0_TOK = [(4096, 6144), (8192, 16416)]
L0_FEAT = [(0, 4096, 0), (6144, 8192, 4096)]
L1_TOK = [(4096, 6912), (6976, 11072)]
L1_FEAT = [(0, 4096, 0), (6912, 6976, 4096)]


def build(stop_after=None, debug=()):
    nc = bass.Bass("TRN2", target_bir_lowering=False)
    g = Ctx()
    g.nc = nc

    def din(name, shape, dt=F32):
        return nc.dram_tensor(name, list(shape), dt, kind="ExternalInput").ap()

    def dscr(name, shape, dt=F32):
        kind = "ExternalOutput" if name in debug else "Internal"
        return nc.dram_tensor(name, list(shape), dt, kind=kind).ap()

    g.xin = din("xin", [NT, D])
    g.cc = din("cc", [128, KC, 2])
    g.ada_w = din("ada_w", [2, D, 3 * D])
    g.ada_b = din("ada_b", [2, 3 * D])
    g.ada_bcol = din("ada_bcol", [2, 128, 32])
    g.ln_g = din("ln_g", [2, D])
    g.ln_b = din("ln_b", [2, D])
    g.ab_w_in = din("ab_w_in", [D, AB_IN])
    g.ab_b_if = din("ab_b_if", [4, 8])
    g.diff_lam = din("diff_lam", [4, 64])
    g.diff_norm_g = din("diff_norm_g", [1, D])
    g.mlstm_norm_g = din("mlstm_norm_g", [1, D])
    g.ab_w_out = din("ab_w_out", [2 * D, D])
    g.cd_w_in = din("cd_w_in", [D, CD_IN])
    g.ret_decay = din("ret_decay", [2, 8])
    g.ret_norm_g = din("ret_norm_g", [1, D])
    g.mla_q_norm_g = din("mla_q_norm_g", [1, 512])
    g.mla_w_uq = din("mla_w_uq", [512, 3072])
    g.mla_kv_norm_g = din("mla_kv_norm_g", [1, 256])
    g.mla_w_ukv = din("mla_w_ukv", [256, 4096])
    g.cd_w_out = din("cd_w_out", [2 * D, D])
    g.cmats = din("cmats", [4, 128, 128])
    g.rope = din("rope", [4, 128, SEQ])
    g.out = nc.dram_tensor("out", [SEQ, D], F32, kind="ExternalOutput").ap()

    g.gate_rows = dscr("gate_rows", [2, 2, D])
    g.ptok0 = dscr("ptok0", [NT, AB_IN])
    g.pfeat0 = dscr("pfeat0", [6144, NT])
    g.ptok1 = dscr("ptok1", [NT, CD_IN])
    g.pfeat1 = dscr("pfeat1", [4160, NT])
    g.xs1 = dscr("xs1", [NT, D])
    g.mix0 = dscr("mix0", [NT, 2 * D])
    g.mix1 = dscr("mix1", [NT, 2 * D])
    g.qupT = dscr("qupT", [3072, NT])
    g.kupT = dscr("kupT", [2048, NT])
    g.vup = dscr("vup", [NT, 4096])
    g.wob = [dscr(f"wob{l}", [128, 32, D], BF16) for l in range(2)]
    g.mixT = [dscr(f"mixT{l}", [NTB, 128, 32, 128], BF16) for l in range(2)]
    g.xsrc = [g.xin, g.xs1]

    with ExitStack() as es:
        S = Sched(nc, es)
        g.S = S
        sb = lambda name, shape, dt=F32: es.enter_context(nc.sbuf_tensor(name, shape, dt))
        g.ident_f = sb("ident_f", [128, 128])
        g.maskU = sb("maskU", [128, 128])
        g.maskL = sb("maskL", [128, 128])
        g.ones_f = sb("ones_f", [128, 128])
        g.ident_b = sb("ident_b", [128, 128], BF16)
        g.modcol = [sb(f"modcol{l}", [128, 32, 2]) for l in range(2)]
        S.dma(g.ident_f[:], g.cmats[0], writes=["c0"])
        S.dma(g.maskU[:], g.cmats[1], writes=["c1"])
        S.dma(g.maskL[:], g.cmats[2], writes=["c2"])
        S.dma(g.ones_f[:], g.cmats[3], writes=["c3"])
        S.dve(lambda e: e.tensor_copy(out=g.ident_b[:], in_=g.ident_f[:]), ["c0"], ["c4"])
        S.flush("consts")

        def done(tag):
            return stop_after == tag

        stages = []

        def run_layer(layer):
            phase_adaln(g, S, layer)
            if done(f"adaln{layer}"):
                return True
            with ExitStack() as es2:
                hT = es2.enter_context(nc.sbuf_tensor(f"hT{layer}", [128, KC, NT], BF16))
                phase_build_hT(g, S, layer, hT)
                if layer == 0:
                    linear_phase(g, S, "inproj0", hT, KC, NT, g.ab_w_in, L0_TOK, L0_FEAT, g.ptok0, g.pfeat0)
                else:
                    linear_phase(g, S, "inproj1", hT, KC, NT, g.cd_w_in, L1_TOK, L1_FEAT, g.ptok1, g.pfeat1)
            if done(f"inproj{layer}"):
                return True
            if layer == 0:
                diff_attn_phase(g, S)
                if done("dattn"):
                    return True
                scan_phase2(g, S, "mlstm", 8, 1, True, g.pfeat0, lambda h, j: 4096 + h * 128, lambda h, j: 5120 + h * 128,
                           g.ptok0, lambda h: 8192 + h * 256, g.mix0, 2048)
                if done("mlstm"):
                    return True
            else:
                with ExitStack() as es2:
                    qnT = es2.enter_context(nc.sbuf_tensor("qnT", [128, 4, NT], BF16))
                    kvnT = es2.enter_context(nc.sbuf_tensor("kvnT", [128, 2, NT], BF16))
                    mla_pre_phase(g, S, qnT, kvnT)
                    linear_phase(g, S, "uq", qnT, 4, NT, g.mla_w_uq, [], [(0, 3072, 0)], None, g.qupT)
                    linear_phase(g, S, "ukv", kvnT, 2, NT, g.mla_w_ukv, [(h * 256 + 128, h * 256 + 256) for h in range(16)],
                                 [(h * 256, h * 256 + 128, h * 128) for h in range(16)], g.vup, g.kupT)
                if done("mlaup"):
                    return True
                mla_attn_phase(g, S)
                if done("mattn"):
                    return True
                scan_phase2(g, S, "ret", 8, 2, False, g.pfeat1, lambda h, j: h * 256 + j * 128, lambda h, j: 2048 + h * 256 + j * 128,
                           g.ptok1, lambda h: 4096 + h * 256, g.mix1, 0)
                if done("ret"):
                    return True
            combine_phase(g, S, layer)
            outproj_phase(g, S, layer)
            return done(f"outp{layer}")

        if MINI is not None:
            MINI(g, S)
        else:
            for layer in range(2):
                if run_layer(layer):
                    break
    print("ops:", S.nops, "signals:", S.cnt)
    return nc


def attn_phase(g, S, name, nheads, load_head, nchunks, scale, qgroups, dst, dst_c0, lam_init=None):
    nc = g.nc
    nmaps = 2 if lam_init is not None else 1
    with ExitStack() as es:
        sb = lambda nm, shape, dt=F32: es.enter_context(nc.sbuf_tensor(f"{name}_{nm}", shape, dt))
        bufs = Ctx()
        bufs.sb = sb
        bufs.vb = [sb(f"vb{i}", [128, NTB, 129], BF16) for i in range(2)]
        bufs.vt = [sb(f"vt{i}", [128, NTB, 128]) for i in range(2)]
        E = [sb(f"E{i}", [128, 512], BF16) for i in range(3)]
        om = [sb(f"om{i}", [128, 4, 129]) for i in range(2)]
        rr = [sb(f"rr{i}", [128, 4]) for i in range(2)]
        ao = [sb(f"ao{i}", [128, 4, 128]) for i in range(2)]
        pss = [es.enter_context(nc.psum_tensor(f"{name}_pss{i}", [128, 512], F32)) for i in range(2)]
        pacc = [es.enter_context(nc.psum_tensor(f"{name}_pacc{i}", [128, 512], F32)) for i in range(4)]
        neglam = sb("neglam", [128, 1])
        if lam_init is not None:
            lamv = sb("lamv", [128, 4, 64])
            lt = sb("lamt", [128, 2, 64])
            ls = sb("lams", [128, 2])
            S.dma(lamv[:], g.diff_lam.rearrange("a b -> (a b)").partition_broadcast(128).rearrange("p (a b) -> p a b", a=4), writes=["lamv"])
            for i in range(2):
                S.dve(lambda e, i=i: e.tensor_tensor(out=lt[:, i, :], in0=lamv[:, 2 * i, :], in1=lamv[:, 2 * i + 1, :], op=ALU.mult), ["lamv"], [("lt", i)])
                S.dve(lambda e, i=i: e.reduce_sum(out=ls[:, i:i + 1], in_=lt[:, i, :], axis=AX.X), [("lt", i)], [("ls", i)])
            S.act(lambda e: e.activation(out=ls[:], in_=ls[:], func=AF.Exp), [("ls", 0), ("ls", 1)], ["lse"])
            S.dve(lambda e: e.tensor_scalar(out=neglam[:], in0=ls[:, 1:2], scalar1=ls[:, 0:1], scalar2=-float(lam_init),
                                            op0=ALU.subtract, op1=ALU.add), ["lse"], ["neglam"])
        for i in range(2):
            S.pool(lambda e, i=i: e.memset(bufs.vb[i][:, :, 128:129], 1.0), [], [f"vb{i}"])
        ui = 0
        gi = 0
        if MINI_HEADS is not None:
            nheads = MINI_HEADS
        pending = load_head(0, bufs, 0)
        for h in range(nheads):
            par = h % 2
            maps = pending
            if h + 1 < nheads:
                pending = load_head(h + 1, bufs, (h + 1) % 2)
            vkey = f"vb{par}"
            vb = bufs.vb[par]
            for (q0, nq, kbs) in qgroups:
                nsub = nq // 128
                ob = gi % 2
                gi += 1
                for m in range(nmaps):
                    for ki, kb in enumerate(kbs):
                        sbk = ui % 2
                        eb = ui % 3
                        ui += 1
                        ops = maps[m]
                        for ci, (kf, qf, keys) in enumerate(ops):
                            S.pe(lambda e, kf=kf, qf=qf, kb=kb, q0=q0, nq=nq, sbk=sbk, ci=ci, n=len(ops):
                                 e.matmul(pss[sbk][:, 0:nq], lhsT=kf(kb * 128, 128), rhs=qf(q0, nq), start=(ci == 0), stop=(ci == n - 1)),
                                 keys, [f"pss{sbk}"])
                        S.act(lambda e, eb=eb, sbk=sbk, nq=nq: e.activation(out=E[eb][:, 0:nq], in_=pss[sbk][:, 0:nq], func=AF.Exp, scale=float(scale)),
                              [f"pss{sbk}"], [f"E{eb}"])
                        for s_ in range(nsub):
                            S.pe(lambda e, eb=eb, s_=s_, kb=kb, vb=vb, first=(ki == 0), last=(ki == len(kbs) - 1):
                                 e.matmul(pacc[s_][:, 0:129], lhsT=E[eb][:, s_ * 128:(s_ + 1) * 128], rhs=vb[:, kb, :], start=first, stop=last),
                                 [f"E{eb}", vkey], [f"pacc{s_}"])
                    mb = m if nmaps == 2 else ob
                    for s_ in range(nsub):
                        _evac(S, s_, om[mb][:, s_, :], pacc[s_][:, 0:129], [f"pacc{s_}"], [(f"om{mb}", s_)])
                    S.dve(lambda e, mb=mb, nsub=nsub: e.reciprocal(out=rr[mb][:, 0:nsub], in_=om[mb][:, 0:nsub, 128]),
                          [(f"om{mb}", s_) for s_ in range(nsub)], [f"rr{mb}"])
                    if m == 1:
                        S.dve(lambda e, nsub=nsub: e.tensor_scalar(out=rr[1][:, 0:nsub], in0=rr[1][:, 0:nsub], scalar1=neglam[:, 0:1], scalar2=None, op0=ALU.mult),
                              ["rr1", "neglam"], ["rr1"])
                aob = ob
                for s_ in range(nsub):
                    if nmaps == 2:
                        S.dve(lambda e, s_=s_, aob=aob: e.tensor_scalar(out=ao[aob][:, s_, :], in0=om[0][:, s_, 0:128], scalar1=rr[0][:, s_:s_ + 1], scalar2=None, op0=ALU.mult),
                              [("om0", s_), "rr0"], [(f"ao{aob}", s_)])
                        S.dve(lambda e, s_=s_, aob=aob: e.scalar_tensor_tensor(out=ao[aob][:, s_, :], in0=om[1][:, s_, 0:128], scalar=rr[1][:, s_:s_ + 1], in1=ao[aob][:, s_, :],
                                                                              op0=ALU.mult, op1=ALU.add),
                              [("om1", s_), "rr1", (f"ao{aob}", s_)], [(f"ao{aob}", s_)])
                    else:
                        S.dve(lambda e, s_=s_, aob=aob, mb=mb: e.tensor_scalar(out=ao[aob][:, s_, :], in0=om[mb][:, s_, 0:128], scalar1=rr[mb][:, s_:s_ + 1], scalar2=None, op0=ALU.mult),
                              [(f"om{mb}", s_), f"rr{mb}"], [(f"ao{aob}", s_)])
                c0 = dst_c0 + h * 128
                S.dma(dst[q0:q0 + nq, c0:c0 + 128].rearrange("(s p) n -> p s n", p=128), ao[aob][:, 0:nsub, :],
                      reads=[(f"ao{aob}", s_) for s_ in range(nsub)])
        S.flush(name)


def rope_fm(S, out_b, x, xs, cosT, sinT, keys_in, key_out, rows, tmp, tmp2):
    S.dve(lambda e: e.tensor_tensor(out=tmp[rows, :], in0=x[rows, CTX:NT], in1=cosT[rows, :], op=ALU.mult), keys_in, ["rope_t1"])
    S.pool(lambda e: e.tensor_tensor(out=tmp2[rows, :], in0=xs[rows, CTX:NT], in1=sinT[rows, :], op=ALU.mult), keys_in, ["rope_t2"])
    S.dve(lambda e: e.tensor_tensor(out=out_b[rows, CTX:NT], in0=tmp[rows, :], in1=tmp2[rows, :], op=ALU.add), ["rope_t1", "rope_t2"], [key_out])
    S.act(lambda e: e.copy(out=out_b[rows, 0:CTX], in_=x[rows, 0:CTX]), keys_in, [key_out + "_c"])


def diff_attn_phase(g, S):
    nc = g.nc
    with ExitStack() as es:
        sb = lambda nm, shape, dt=F32: es.enter_context(nc.sbuf_tensor(f"da_{nm}", shape, dt))
        cosT = sb("cos", [128, SEQ])
        sinT = sb("sin", [128, SEQ])
        S.dma(cosT[:], g.rope[0], writes=["cos"])
        S.dma(sinT[:], g.rope[1], writes=["sin"])
        xf = [sb(f"xf{i}", [128, NT]) for i in range(2)]
        xs = [sb(f"xs{i}", [128, NT]) for i in range(2)]
        t1 = sb("t1", [128, SEQ])
        t2 = sb("t2", [128, SEQ])
        qb = [sb(f"qb{i}", [128, NT], BF16) for i in range(2)]
        kb_ = [sb(f"kb{i}", [128, NT], BF16) for i in range(2)]

        def load_head(h, bufs, par):
            for which, (dstb, r0) in enumerate(((qb[par], h * 128), (kb_[par], 2048 + h * 128))):
                S.dma(xf[which][:], g.pfeat0[r0:r0 + 128, :], writes=[f"xf{which}"])
                for a, b in ((0, 32), (32, 0), (64, 96), (96, 64)):
                    S.dma(xs[which][a:a + 32, :], g.pfeat0[r0 + b:r0 + b + 32, :], writes=[(f"xs{which}", a)])
                kin = [f"xf{which}"] + [(f"xs{which}", a) for a in (0, 32, 64, 96)] + ["cos", "sin"]
                kname = ("qb" if which == 0 else "kb") + str(par)
                rope_fm(S, dstb, xf[which], xs[which], cosT, sinT, kin, kname, slice(0, 128), t1, t2)
            vt = bufs.vt[par]
            vb = bufs.vb[par]
            S.dma(vt[:], g.ptok0[:, 4096 + h * 128:4096 + (h + 1) * 128].rearrange("(t p) n -> p t n", p=128), writes=[f"vt{par}"])
            S.pool(lambda e: e.tensor_copy(out=vb[:, :, 0:128], in_=vt[:]), [f"vt{par}"], [f"vb{par}"])
            qk = [f"qb{par}", f"qb{par}_c", f"kb{par}", f"kb{par}_c"]
            maps = []
            for m in range(2):
                rows = slice(m * 64, (m + 1) * 64)
                maps.append([(lambda k0, n, rows=rows, kk=kb_[par]: kk[rows, k0:k0 + n],
                              lambda q0, n, rows=rows, qq=qb[par]: qq[rows, q0:q0 + n], qk)])
            return maps

        qgroups = [(0, 256, [0, 1])] + [(CTX + i * 512, 512, list(range(NTB))) for i in range(4)]
        attn_phase(g, S, "dattn", 16, load_head, NTB, 64 ** -0.5, qgroups, g.mix0, 0, lam_init=0.2)


def mla_attn_phase(g, S):
    nc = g.nc
    with ExitStack() as es:
        sb = lambda nm, shape, dt=F32: es.enter_context(nc.sbuf_tensor(f"ma_{nm}", shape, dt))
        cosT = sb("cos", [128, SEQ])
        sinT = sb("sin", [128, SEQ])
        S.dma(cosT[:], g.rope[0], writes=["cos"])
        S.dma(sinT[:], g.rope[1], writes=["sin"])
        xf = [sb(f"xf{i}", [128, NT]) for i in range(3)]
        xs = sb("xs", [64, NT])
        t1 = sb("t1", [128, SEQ])
        t2 = sb("t2", [128, SEQ])
        qn = [sb(f"qn{i}", [128, NT], BF16) for i in range(2)]
        qr = [sb(f"qr{i}", [64, NT], BF16) for i in range(2)]
        kn = [sb(f"kn{i}", [128, NT], BF16) for i in range(2)]
        kr = sb("kr", [64, NT], BF16)
        S.dma(xf[1][0:64, :], g.pfeat1[4096:4160, :], writes=["xf1"])
        S.dma(xs[0:32, :], g.pfeat1[4128:4160, :], writes=[("xs", 0)])
        S.dma(xs[32:64, :], g.pfeat1[4096:4128, :], writes=[("xs", 32)])
        rope_fm(S, kr, xf[1], xs, cosT, sinT, ["xf1", ("xs", 0), ("xs", 32), "cos", "sin"], "kr", slice(0, 64), t1, t2)

        def load_head(h, bufs, par):
            r = h * 192
            S.dma(xf[0][:], g.qupT[r:r + 128, :], writes=["xf0"])
            S.act(lambda e: e.copy(out=qn[par][:], in_=xf[0][:]), ["xf0"], [f"qn{par}"])
            S.dma(xf[1][0:64, :], g.qupT[r + 128:r + 192, :], writes=["xf1"])
            S.dma(xs[0:32, :], g.qupT[r + 160:r + 192, :], writes=[("xs", 0)])
            S.dma(xs[32:64, :], g.qupT[r + 128:r + 160, :], writes=[("xs", 32)])
            rope_fm(S, qr[par], xf[1], xs, cosT, sinT, ["xf1", ("xs", 0), ("xs", 32), "cos", "sin"], f"qr{par}", slice(0, 64), t1, t2)
            S.dma(xf[2][:], g.kupT[h * 128:(h + 1) * 128, :], writes=["xf2"])
            S.dve(lambda e: e.tensor_copy(out=kn[par][:], in_=xf[2][:]), ["xf2"], [f"kn{par}"])
            vt = bufs.vt[par]
            vb = bufs.vb[par]
            S.dma(vt[:], g.vup[:, h * 256 + 128:(h + 1) * 256].rearrange("(t p) n -> p t n", p=128), writes=[f"vt{par}"])
            S.pool(lambda e: e.tensor_copy(out=vb[:, :, 0:128], in_=vt[:]), [f"vt{par}"], [f"vb{par}"])
            keys = [f"qn{par}", f"qr{par}", f"qr{par}_c", f"kn{par}", "kr", "kr_c"]
            return [[(lambda k0, n, kk=kn[par]: kk[:, k0:k0 + n], lambda q0, n, qq=qn[par]: qq[:, q0:q0 + n], keys),
                     (lambda k0, n: kr[0:64, k0:k0 + n], lambda q0, n, qq=qr[par]: qq[0:64, q0:q0 + n], keys)]]

        qgroups = [(CTX + i * 512, 512, list(range(NTB))) for i in range(4)]
        attn_phase(g, S, "mattn", 16, load_head, NTB, 192 ** -0.5, qgroups, g.mix1, 2048, lam_init=None)


def scan_phase(g, S, name, nheads, nj, gated, pfeat, qrow, krow, ptok, vcol, dst, dst_c0):
    nc = g.nc
    aug = 257 if gated else 256
    with ExitStack() as es:
        sb = lambda nm, shape, dt=F32: es.enter_context(nc.sbuf_tensor(f"{name}_{nm}", shape, dt))
        ps_ = lambda nm, shape, dt=F32: es.enter_context(nc.psum_tensor(f"{name}_{nm}", shape, dt))
        xq = [sb(f"xq{j}", [128, NT]) for j in range(nj)]
        xk = [sb(f"xk{j}", [128, NT]) for j in range(nj)]
        vt = sb("vt", [128, NTB, 256])
        qTs = [[sb(f"qTs{p}{j}", [128, NT]) for j in range(nj)] for p in range(2)]
        kTb = [[sb(f"kTb{p}{j}", [128, NT], BF16) for j in range(nj)] for p in range(2)]
        vb = [sb(f"vb{p}", [128, NTB, aug], BF16) for p in range(2)]
        hacc = [sb(f"hacc{p}", [128, NTB, 256]) for p in range(2)]
        Cst = [[sb(f"Cst{d}{j}", [128, aug]) for j in range(nj)] for d in range(2)]
        Cb = [[sb(f"Cb{d}{j}", [128, aug], BF16) for j in range(nj)] for d in range(2)]
        lfrep = [sb(f"lfrep{d}", [128, 128]) for d in range(2)]
        abc = [sb(f"abc{d}", [128, 128]) for d in range(2)]
        ucol = [sb(f"u{d}", [128, 1]) for d in range(2)]
        wcol = [sb(f"w{d}", [128, 1]) for d in range(2)]
        rcol = [sb(f"r{d}", [128, 1]) for d in range(2)]
        Qt = [[sb(f"Qt{d}{j}", [128, 128], BF16) for j in range(nj)] for d in range(2)]
        Kw = [sb(f"Kw{d}", [128, nj, 128], BF16) for d in range(2)]
        St = [sb(f"St{d}", [128, 128], BF16) for d in range(2)]
        p_bb = ps_("pbb", [128, 512])
        p_g = ps_("pg", [128, 512])
        p_t = ps_("pt", [128, nj, 128], BF16)
        p_h = [ps_(f"ph{d}", [128, 512]) for d in range(2)]
        p_c = [ps_(f"pc{d}", [128, nj, 256 if nj == 2 else 512]) for d in range(2)]
        onec = g.ones_f[:, 0:1]
        if gated:
            G = sb("G", [128, NTB, 32])
            bif = sb("bif", [128, 32])
            tmpg = sb("tmpg", [128, NTB, 16])
            S.dma(G[:], ptok[:, 12288:12320].rearrange("(t p) n -> p t n", p=128), writes=["G"])
            S.dma(bif[:], g.ab_b_if.rearrange("a b -> (a b)").partition_broadcast(128), writes=["bif"])
            for tb in range(NTB):
                S.dve(lambda e, tb=tb: e.tensor_tensor(out=G[:, tb, :], in0=G[:, tb, :], in1=bif[:], op=ALU.add), ["G", "bif"], ["G"])
            S.act(lambda e: e.activation(out=tmpg[:], in_=G[:, :, 16:32], func=AF.Exp, scale=-1.0), ["G"], ["tmpg"])
            S.act(lambda e: e.activation(out=tmpg[:], in_=tmpg[:], func=AF.Ln, bias=onec), ["tmpg"], ["tmpg"])
            S.dve(lambda e: e.tensor_scalar(out=G[:, :, 16:32], in0=tmpg[:], scalar1=-1.0, scalar2=None, op0=ALU.mult), ["tmpg", "G"], ["G"])
            for p in range(2):
                S.pool(lambda e, p=p: e.memset(vb[p][:, :, 256:257], 1.0), [], [f"vb{p}"])
        else:
            cosT = sb("cos", [128, SEQ])
            sinT = sb("sin", [128, SEQ])
            t1 = sb("t1", [128, SEQ])
            t2 = sb("t2", [128, SEQ])
            S.dma(cosT[:], g.rope[2], writes=["cos"])
            S.dma(sinT[:], g.rope[3], writes=["sin"])
            lgt = sb("lgt", [128, 16])
            kbias = sb("kbias", [128, 1])
            zcol = sb("zcol", [128, 1])
            S.dma(lgt[:], g.ret_decay.rearrange("a b -> (a b)").partition_broadcast(128), writes=["lgt"])
            S.act(lambda e: e.activation(out=lgt[:], in_=lgt[:], func=AF.Exp, scale=-1.0), ["lgt"], ["lgt"])
            S.act(lambda e: e.activation(out=lgt[:], in_=lgt[:], func=AF.Ln, bias=onec), ["lgt"], ["lgt"])
            S.dve(lambda e: e.tensor_scalar(out=lgt[:], in0=lgt[:], scalar1=-1.0, scalar2=None, op0=ALU.mult), ["lgt"], ["lgt"])
            S.pool(lambda e: e.memset(kbias[:], math.log(1.0 / 16.0)), [], ["kbias"])

        def gate_consts(d, lf_col, gi_col, lfkeys):
            M = g.maskU if d == 0 else g.maskL
            jl = 127 if d == 0 else 0
            S.pool(lambda e: e.tensor_scalar(out=lfrep[d][:], in0=g.ones_f[:], scalar1=lf_col, scalar2=None, op0=ALU.mult), lfkeys, [f"lfrep{d}"])
            S.pe(lambda e: e.matmul(p_bb[:, 0:128], lhsT=lfrep[d][:], rhs=M[:], start=True, stop=True), [f"lfrep{d}"], ["pbb"])
            S.pe(lambda e: e.matmul(p_bb[:, 128:129], lhsT=M[:], rhs=lf_col, start=True, stop=True), lfkeys, ["pbb"])
            S.act(lambda e: e.activation(out=abc[d][:], in_=p_bb[:, 0:128], func=AF.Exp), ["pbb"], [f"abc{d}"])
            S.act(lambda e: e.activation(out=ucol[d][:], in_=p_bb[:, 128:129], func=AF.Exp, scale=-1.0, bias=gi_col), ["pbb"] + lfkeys, [f"u{d}"])
            S.dve(lambda e: e.tensor_tensor(out=wcol[d][:], in0=ucol[d][:], in1=abc[d][:, jl:jl + 1], op=ALU.mult), [f"u{d}", f"abc{d}"], [f"w{d}"])

        def load_head(h, p):
            for j in range(nj):
                S.dma(xq[j][:], pfeat[qrow(h, j):qrow(h, j) + 128, :], writes=[f"xq{j}"])
                S.dma(xk[j][:], pfeat[krow(h, j):krow(h, j) + 128, :], writes=[f"xk{j}"])
            if gated:
                S.act(lambda e: e.mul(out=qTs[p][0][:], in_=xq[0][:], mul=128 ** -0.5), ["xq0"], [f"qTs{p}0"])
                S.dve(lambda e: e.tensor_copy(out=kTb[p][0][:], in_=xk[0][:]), ["xk0"], [f"kTb{p}0"])
            else:
                lat = slice(CTX, NT)
                for (src, dstl, nm) in ((xq, qTs[p], f"qTs{p}"), (xk, kTb[p], f"kTb{p}")):
                    sk = [("xq0", "xq1") if src is xq else ("xk0", "xk1")][0]
                    S.dve(lambda e, src=src: e.tensor_tensor(out=t1[:], in0=src[0][:, lat], in1=cosT[:], op=ALU.mult), [sk[0], "cos"], ["t1"])
                    S.pool(lambda e, src=src: e.tensor_tensor(out=t2[:], in0=src[1][:, lat], in1=sinT[:], op=ALU.mult), [sk[1], "sin"], ["t2"])
                    S.dve(lambda e, dstl=dstl: e.tensor_tensor(out=dstl[0][:, lat], in0=t1[:], in1=t2[:], op=ALU.subtract), ["t1", "t2"], [nm + "0"])
                    S.dve(lambda e, src=src: e.tensor_tensor(out=t1[:], in0=src[0][:, lat], in1=sinT[:], op=ALU.mult), [sk[0], "sin"], ["t1"])
                    S.pool(lambda e, src=src: e.tensor_tensor(out=t2[:], in0=src[1][:, lat], in1=cosT[:], op=ALU.mult), [sk[1], "cos"], ["t2"])
                    S.dve(lambda e, dstl=dstl: e.tensor_tensor(out=dstl[1][:, lat], in0=t1[:], in1=t2[:], op=ALU.add), ["t1", "t2"], [nm + "1"])
                    for j in range(2):
                        S.act(lambda e, src=src, dstl=dstl, j=j: e.copy(out=dstl[j][:, 0:CTX], in_=src[j][:, 0:CTX]), [sk[j]], [nm + str(j) + "c"])
            S.dma(vt[:], ptok[:, vcol(h):vcol(h) + 256].rearrange("(t p) n -> p t n", p=128), writes=["vt"])
            S.pool(lambda e: e.tensor_copy(out=vb[p][:, :, 0:256], in_=vt[:]), ["vt"], [f"vb{p}"])

        def do_head(h, p):
            qkeys = [f"qTs{p}{j}" for j in range(nj)] + ([f"qTs{p}{j}c" for j in range(nj)] if not gated else [])
            kkeys = [f"kTb{p}{j}" for j in range(nj)] + ([f"kTb{p}{j}c" for j in range(nj)] if not gated else [])
            for d in range(2):
                for j in range(nj):
                    S.pool(lambda e, d=d, j=j: e.memset(Cst[d][j][:], 0.0), [], [f"Cst{d}{j}"])
                    S.pool(lambda e, d=d, j=j: e.memset(Cb[d][j][:], 0.0), [], [f"Cb{d}{j}"])
                if not gated:
                    gate_consts(d, lgt[:, d * 8 + h:d * 8 + h + 1], kbias[:, 0:1], ["lgt", "kbias"])
            order = [list(range(NTB)), [1, 0] + list(range(NTB - 1, 1, -1))]
            written = set()

            def step(d, tb):
                cols = slice(tb * 128, (tb + 1) * 128)
                M = g.maskU if d == 0 else g.maskL
                jl = 127 if d == 0 else 0
                if gated:
                    ci = d * 8 + h
                    gate_consts(d, G[:, tb, 16 + ci:17 + ci], G[:, tb, ci:ci + 1], ["G"])
                for j in range(nj):
                    S.dve(lambda e, d=d, j=j, cols=cols: e.tensor_tensor(out=Qt[d][j][:], in0=qTs[p][j][:, cols], in1=abc[d][:], op=ALU.mult),
                          qkeys + [f"abc{d}"], [f"Qt{d}{j}"])
                for j in range(nj):
                    S.pe(lambda e, j=j, cols=cols: e.transpose(p_t[:, j, :], kTb[p][j][:, cols], g.ident_b[:]), kkeys, ["pt"])
                S.act(lambda e, d=d: e.activation(out=Kw[d][:], in_=p_t[:], func=AF.Identity, scale=wcol[d][:, 0:1]), ["pt", f"w{d}"], [f"Kw{d}"])
                for j in range(nj):
                    S.pe(lambda e, d=d, j=j, cols=cols: e.matmul(p_g[:, 0:128], lhsT=kTb[p][j][:, cols], rhs=Qt[d][j][:], start=(j == 0), stop=(j == nj - 1)),
                         kkeys + [f"Qt{d}{j}"], ["pg"])
                S.dve(lambda e, d=d, M=M: e.scalar_tensor_tensor(out=St[d][:], in0=p_g[:, 0:128], scalar=ucol[d][:, 0:1], in1=M[:], op0=ALU.mult, op1=ALU.mult),
                      ["pg", f"u{d}"], [f"St{d}"])
                for j in range(nj):
                    S.pe(lambda e, d=d, j=j: e.matmul(p_h[d][:, 0:aug], lhsT=Qt[d][j][:], rhs=Cb[d][j][:, 0:aug], start=(j == 0), stop=False),
                         [f"Qt{d}{j}", f"Cb{d}{j}"], [f"ph{d}"])
                S.pe(lambda e, d=d, tb=tb: e.matmul(p_h[d][:, 0:aug], lhsT=St[d][:], rhs=vb[p][:, tb, 0:aug], start=False, stop=True),
                     [f"St{d}", f"vb{p}"], [f"ph{d}"])
                hk = (f"hacc{p}", tb)
                if gated:
                    S.act(lambda e, d=d: e.activation(out=rcol[d][:], in_=p_h[d][:, 256:257], func=AF.Abs), [f"ph{d}"], [f"r{d}"])
                    S.dve(lambda e, d=d: e.tensor_scalar(out=rcol[d][:], in0=rcol[d][:], scalar1=1.0, scalar2=None, op0=ALU.max), [f"r{d}"], [f"r{d}"])
                    S.dve(lambda e, d=d: e.reciprocal(out=rcol[d][:], in_=rcol[d][:]), [f"r{d}"], [f"r{d}"])
                    if tb not in written:
                        S.dve(lambda e, d=d, tb=tb: e.tensor_scalar(out=hacc[p][:, tb, :], in0=p_h[d][:, 0:256], scalar1=rcol[d][:, 0:1], scalar2=None, op0=ALU.mult),
                              [f"ph{d}", f"r{d}"], [hk])
                    else:
                        S.dve(lambda e, d=d, tb=tb: e.scalar_tensor_tensor(out=hacc[p][:, tb, :], in0=p_h[d][:, 0:256], scalar=rcol[d][:, 0:1], in1=hacc[p][:, tb, :],
                                                                            op0=ALU.mult, op1=ALU.add), [f"ph{d}", f"r{d}", hk], [hk])
                else:
                    if tb not in written:
                        S.act(lambda e, d=d, tb=tb: e.copy(out=hacc[p][:, tb, :], in_=p_h[d][:, 0:256]), [f"ph{d}"], [hk])
                    else:
                        S.dve(lambda e, d=d, tb=tb: e.tensor_tensor(out=hacc[p][:, tb, :], in0=p_h[d][:, 0:256], in1=hacc[p][:, tb, :], op=ALU.add), [f"ph{d}", hk], [hk])
                written.add(tb)
                for j in range(nj):
                    S.pe(lambda e, d=d, j=j, tb=tb: e.matmul(p_c[d][:, j, 0:aug], lhsT=Kw[d][:, j, :], rhs=vb[p][:, tb, 0:aug], start=True, stop=True),
                         [f"Kw{d}", f"vb{p}"], [f"pc{d}"])
                for j in range(nj):
                    S.dve(lambda e, d=d, j=j, jl=jl: e.scalar_tensor_tensor(out=Cst[d][j][:], in0=Cst[d][j][:], scalar=abc[d][:, jl:jl + 1], in1=p_c[d][:, j, 0:aug],
                                                                           op0=ALU.mult, op1=ALU.add), [f"Cst{d}{j}", f"abc{d}", f"pc{d}"], [f"Cst{d}{j}"])
                    S.act(lambda e, d=d, j=j: e.copy(out=Cb[d][j][:], in_=Cst[d][j][:]), [f"Cst{d}{j}"], [f"Cb{d}{j}"])

            for i in range(NTB):
                for d in range(2):
                    step(d, order[d][i])
            c0 = dst_c0 + h * 256
            S.dma(dst[:, c0:c0 + 256].rearrange("(t p) n -> p t n", p=128), hacc[p][:], reads=[(f"hacc{p}", tb) for tb in range(NTB)])

        load_head(0, 0)
        for h in range(nheads):
            if h + 1 < nheads:
                load_head(h + 1, (h + 1) % 2)
            do_head(h, h % 2)
        S.flush(name)


def group_norm(S, nc, x3, G, d, mean_center, tmp, st, xkey, tagp):
    s_sum, s_sq, s_mean, s_rstd = st
    k0, k1, k2, k3 = (tagp + "s0", tagp + "s1", tagp + "s2", tagp + "s3")
    S.dve(lambda e: e.tensor_tensor(out=tmp, in0=x3, in1=x3, op=ALU.mult), [xkey], ["tmp"])
    S.dve(lambda e: e.reduce_sum(out=s_sq, in_=tmp, axis=AX.X), ["tmp"], [k1])
    if mean_center:
        S.dve(lambda e: e.reduce_sum(out=s_sum, in_=x3, axis=AX.X), [xkey], [k0])
        S.dve(lambda e: e.tensor_scalar(out=s_mean, in0=s_sum, scalar1=1.0 / d, scalar2=None, op0=ALU.mult), [k0], [k2])
        S.dve(lambda e: e.tensor_tensor(out=s_sum, in0=s_mean, in1=s_mean, op=ALU.mult), [k2], [k0])
        S.dve(lambda e: e.scalar_tensor_tensor(out=s_sq, in0=s_sq, scalar=1.0 / d, in1=s_sum, op0=ALU.mult, op1=ALU.subtract), [k1, k0], [k1])
        S.dve(lambda e: e.tensor_scalar(out=s_sq, in0=s_sq, scalar1=EPS, scalar2=None, op0=ALU.add), [k1], [k1])
    else:
        S.dve(lambda e: e.tensor_scalar(out=s_sq, in0=s_sq, scalar1=1.0 / d, scalar2=EPS, op0=ALU.mult, op1=ALU.add), [k1], [k1])
    S.act(lambda e: e.activation(out=s_rstd, in_=s_sq, func=AF.Sqrt), [k1], [k3])
    S.dve(lambda e: e.reciprocal(out=s_rstd, in_=s_rstd), [k3], [k3])
    for gi in range(G):
        if mean_center:
            S.dve(lambda e, gi=gi: e.tensor_scalar(out=x3[:, gi, :], in0=x3[:, gi, :], scalar1=s_mean[:, gi:gi + 1], scalar2=s_rstd[:, gi:gi + 1],
                                                   op0=ALU.subtract, op1=ALU.mult), [xkey, k2, k3], [xkey])
        else:
            S.dve(lambda e, gi=gi: e.tensor_scalar(out=x3[:, gi, :], in0=x3[:, gi, :], scalar1=s_rstd[:, gi:gi + 1], scalar2=None, op0=ALU.mult),
                  [xkey, k3], [xkey])


def wout_convert_phase(g, S, layer):
    nc = g.nc
    W = (g.ab_w_out if layer == 0 else g.cd_w_out).rearrange("(k p) n -> p k n", p=128)
    with ExitStack() as es:
        wf = [es.enter_context(nc.sbuf_tensor(f"wc{layer}_f{i}", [128, 8, 512], F32)) for i in range(2)]
        wb = [es.enter_context(nc.sbuf_tensor(f"wc{layer}_b{i}", [128, 8, 512], BF16)) for i in range(2)]
        i = 0
        for cg in range(4):
            for k0 in range(0, 32, 8):
                b = i % 2
                i += 1
                S.dma(wf[b][:], W[:, k0:k0 + 8, cg * 512:(cg + 1) * 512], writes=[f"wf{b}"])
                if i % 2 == 0:
                    S.pool(lambda e, b=b: e.tensor_copy(out=wb[b][:], in_=wf[b][:]), [f"wf{b}"], [f"wb{b}"])
                else:
                    S.act(lambda e, b=b: e.copy(out=wb[b][:], in_=wf[b][:]), [f"wf{b}"], [f"wb{b}"])
                S.dma(g.wob[layer][:, k0:k0 + 8, cg * 512:(cg + 1) * 512], wb[b][:], reads=[f"wb{b}"])
        S.flush(f"wconv{layer}")


def out_phase(g, S, layer):
    nc = g.nc
    name = f"outp{layer}"
    ptok = g.ptok0 if layer == 0 else g.ptok1
    mix = g.mix0 if layer == 0 else g.mix1
    gate_c0 = 12320 if layer == 0 else 6976
    Wout = g.wob[layer]
    tbs = list(range(NTB)) if layer == 0 else list(range(2, NTB))
    with ExitStack() as es:
        sb = lambda nm, shape, dt=F32: es.enter_context(nc.sbuf_tensor(f"{name}_{nm}", shape, dt))
        ps_ = lambda nm, shape, dt=F32: es.enter_context(nc.psum_tensor(f"{name}_{nm}", shape, dt))
        mr = sb("mr", [128, 4096])
        gt = sb("gt", [128, 4096])
        og = sb("og", [128, 2048])
        tmp = sb("tmp", [128, 2048])
        gbc = sb("gbc", [128, 4096])
        mixb = sb("mixb", [128, 4096], BF16)
        mixT = sb("mixT", [128, 32, 128], BF16)
        wt = [sb(f"wt{i}", [128, 32, 512], BF16) for i in range(2)]
        xb = sb("xb", [128, D])
        z = sb("z", [128, D])
        grow = [sb(f"grow{w}", [128, D]) for w in range(2)]
        lng = sb("lng", [128, D])
        lnb = sb("lnb", [128, D])
        st = [sb(f"st{i}", [128, 16]) for i in range(4)]
        st2 = [sb(f"stb{i}", [128, 16]) for i in range(4)]
        ls = [sb(f"ls{i}", [128, 1]) for i in range(4)]
        p_t = [ps_(f"pt{i}", [128, 4, 128], BF16) for i in range(2)]
        p_y = [ps_(f"py{i}", [128, 512]) for i in range(4)]
        if layer == 0:
            S.dma(gbc[:, 0:2048], g.diff_norm_g.partition_broadcast(128), writes=["gbc"])
            S.dma(gbc[:, 2048:4096], g.mlstm_norm_g.partition_broadcast(128), writes=["gbc2"])
            S.dve(lambda e: e.tensor_scalar(out=gbc[:, 0:2048], in0=gbc[:, 0:2048], scalar1=0.8, scalar2=None, op0=ALU.mult), ["gbc"], ["gbc"])
        else:
            S.dma(gbc[:, 0:2048], g.ret_norm_g.partition_broadcast(128), writes=["gbc"])
        for w in range(2):
            S.dma(grow[w][:], g.gate_rows[layer, w:w + 1, :].partition_broadcast(128), writes=[f"grow{w}"])
        S.dma(lng[:], g.ln_g[layer:layer + 1, :].partition_broadcast(128), writes=["lng"])
        S.dma(lnb[:], g.ln_b[layer:layer + 1, :].partition_broadcast(128), writes=["lnb"])
        wi = 0
        for tb in tbs:
            rows = slice(tb * 128, (tb + 1) * 128)
            w = 1 if tb < 2 else 0
            S.dma(mr[:], mix[rows, :], writes=["mr"])
            S.dma(gt[:], ptok[rows, gate_c0:gate_c0 + 4096], writes=["gt"])
            S.dma(xb[:], g.xsrc[layer][rows, :], writes=["xb"])
            if layer == 0:
                S.dma(og[:], ptok[rows, 10240:12288], writes=["og"])
                group_norm(S, nc, mr[:, 0:2048].rearrange("p (g d) -> p g d", d=128), 16, 128, False,
                           tmp[:].rearrange("p (g d) -> p g d", d=128), [s_[:, 0:16] for s_ in st], "mr", "nA")
                group_norm(S, nc, mr[:, 2048:4096].rearrange("p (g d) -> p g d", d=256), 8, 256, True,
                           tmp[:].rearrange("p (g d) -> p g d", d=256), [s_[:, 0:8] for s_ in st2], "mr", "nB")
                S.act(lambda e: e.activation(out=og[:], in_=og[:], func=AF.Sigmoid), ["og"], ["og"])
                S.dve(lambda e: e.tensor_tensor(out=mr[:], in0=mr[:], in1=gbc[:], op=ALU.mult), ["mr", "gbc", "gbc2"], ["mr"])
                S.dve(lambda e: e.tensor_tensor(out=mr[:, 2048:4096], in0=mr[:, 2048:4096], in1=og[:], op=ALU.mult), ["mr", "og"], ["mr"])
            else:
                group_norm(S, nc, mr[:, 0:2048].rearrange("p (g d) -> p g d", d=256), 8, 256, True,
                           tmp[:].rearrange("p (g d) -> p g d", d=256), [s_[:, 0:8] for s_ in st2], "mr", "nA")
                S.dve(lambda e: e.tensor_tensor(out=mr[:, 0:2048], in0=mr[:, 0:2048], in1=gbc[:, 0:2048], op=ALU.mult), ["mr", "gbc"], ["mr"])
            S.act(lambda e: e.activation(out=gt[:], in_=gt[:], func=AF.Silu), ["gt"], ["gt"])
            S.dve(lambda e: e.tensor_tensor(out=mixb[:], in0=mr[:], in1=gt[:], op=ALU.mult), ["mr", "gt"], ["mixb"])
            for q in range(8):
                pb = q % 2
                for i in range(4):
                    kc = q * 4 + i
                    S.pe(lambda e, kc=kc, pb=pb, i=i: e.transpose(p_t[pb][:, i, :], mixb[:, kc * 128:(kc + 1) * 128], g.ident_b[:]), ["mixb"], [f"pt{pb}"])
                _evac(S, q, mixT[:, q * 4:(q + 1) * 4, :], p_t[pb][:], [f"pt{pb}"], [("mixT", q)])
            for cg in range(4):
                b = wi % 2
                wi += 1
                S.dma(wt[b][:], Wout[:, :, cg * 512:(cg + 1) * 512], writes=[f"wt{b}"])
                for kc in range(32):
                    S.pe(lambda e, b=b, kc=kc, cg=cg: e.matmul(p_y[cg][:], lhsT=mixT[:, kc, :], rhs=wt[b][:, kc, :], start=(kc == 0), stop=(kc == 31)),
                         [f"wt{b}", ("mixT", kc // 4)], [f"py{cg}"])
                S.dve(lambda e, cg=cg, w=w: e.tensor_tensor(out=z[:, cg * 512:(cg + 1) * 512], in0=p_y[cg][:], in1=grow[w][:, cg * 512:(cg + 1) * 512], op=ALU.mult),
                      [f"py{cg}", f"grow{w}"], [("z", cg)])
            zk = [("z", cg) for cg in range(4)]
            S.dve(lambda e: e.scalar_tensor_tensor(out=z[:], in0=xb[:], scalar=float(ALPHA), in1=z[:], op0=ALU.mult, op1=ALU.add), zk + ["xb"], zk)
            S.dve(lambda e: e.reduce_sum(out=ls[0][:], in_=z[:], axis=AX.X), zk, ["ls0"])
            S.dve(lambda e: e.tensor_scalar(out=ls[0][:], in0=ls[0][:], scalar1=1.0 / D, scalar2=None, op0=ALU.mult), ["ls0"], ["ls0"])
            S.dve(lambda e: e.tensor_scalar(out=z[:], in0=z[:], scalar1=ls[0][:, 0:1], scalar2=None, op0=ALU.subtract), zk + ["ls0"], zk)
            S.dve(lambda e: e.tensor_tensor(out=tmp[:], in0=z[:], in1=z[:], op=ALU.mult), zk, ["tmp"])
            S.dve(lambda e: e.reduce_sum(out=ls[1][:], in_=tmp[:], axis=AX.X), ["tmp"], ["ls1"])
            S.dve(lambda e: e.tensor_scalar(out=ls[1][:], in0=ls[1][:], scalar1=1.0 / D, scalar2=EPS, op0=ALU.mult, op1=ALU.add), ["ls1"], ["ls1"])
            S.act(lambda e: e.activation(out=ls[2][:], in_=ls[1][:], func=AF.Sqrt), ["ls1"], ["ls2"])
            S.dve(lambda e: e.reciprocal(out=ls[2][:], in_=ls[2][:]), ["ls2"], ["ls2"])
            S.dve(lambda e: e.scalar_tensor_tensor(out=z[:], in0=z[:], scalar=ls[2][:, 0:1], in1=lng[:], op0=ALU.mult, op1=ALU.mult), zk + ["ls2", "lng"], zk)
            S.dve(lambda e: e.tensor_tensor(out=z[:], in0=z[:], in1=lnb[:], op=ALU.add), zk + ["lnb"], zk)
            if layer == 0:
                S.dma(g.xs1[rows, :], z[:], reads=zk)
            else:
                S.dma(g.out[(tb - 2) * 128:(tb - 1) * 128, :], z[:], reads=zk)
        S.flush(name)


def mla_pre_phase(g, S, qnT, kvnT):
    nc = g.nc
    with ExitStack() as es:
        sb = lambda nm, shape, dt=F32: es.enter_context(nc.sbuf_tensor(f"mp_{nm}", shape, dt))
        xl = sb("xl", [128, 768])
        tmp = sb("tmp", [128, 768])
        gb = sb("gb", [128, 768])
        xbf = sb("xbf", [128, 768])
        ss = sb("ss", [128, 2])
        p_t = [es.enter_context(nc.psum_tensor(f"mp_pt{i}", [128, 4, 128], F32)) for i in range(2)]
        p_u = [es.enter_context(nc.psum_tensor(f"mp_pu{i}", [128, 2, 128], F32)) for i in range(2)]
        S.dma(gb[:, 0:512], g.mla_q_norm_g.partition_broadcast(128), writes=["gb"])
        S.dma(gb[:, 512:768], g.mla_kv_norm_g.partition_broadcast(128), writes=["gb2"])
        for tb in range(NTB):
            pb = tb % 2
            S.dma(xl[:], g.ptok1[tb * 128:(tb + 1) * 128, 6144:6912], writes=["xl"])
            S.dve(lambda e: e.tensor_tensor(out=tmp[:], in0=xl[:], in1=xl[:], op=ALU.mult), ["xl"], ["tmp"])
            S.dve(lambda e: e.reduce_sum(out=ss[:, 0:1], in_=tmp[:, 0:512], axis=AX.X), ["tmp"], ["ss"])
            S.dve(lambda e: e.reduce_sum(out=ss[:, 1:2], in_=tmp[:, 512:768], axis=AX.X), ["tmp", "ss"], ["ss"])
            S.dve(lambda e: e.tensor_scalar(out=ss[:, 0:1], in0=ss[:, 0:1], scalar1=1.0 / 512, scalar2=EPS, op0=ALU.mult, op1=ALU.add), ["ss"], ["ss"])
            S.dve(lambda e: e.tensor_scalar(out=ss[:, 1:2], in0=ss[:, 1:2], scalar1=1.0 / 256, scalar2=EPS, op0=ALU.mult, op1=ALU.add), ["ss"], ["ss"])
            S.act(lambda e: e.activation(out=ss[:], in_=ss[:], func=AF.Sqrt), ["ss"], ["ss"])
            S.dve(lambda e: e.reciprocal(out=ss[:], in_=ss[:]), ["ss"], ["ss"])
            S.dve(lambda e: e.scalar_tensor_tensor(out=xbf[:, 0:512], in0=xl[:, 0:512], scalar=ss[:, 0:1], in1=gb[:, 0:512], op0=ALU.mult, op1=ALU.mult),
                  ["xl", "ss", "gb"], ["xbf"])
            S.dve(lambda e: e.scalar_tensor_tensor(out=xbf[:, 512:768], in0=xl[:, 512:768], scalar=ss[:, 1:2], in1=gb[:, 512:768], op0=ALU.mult, op1=ALU.mult),
                  ["xl", "ss", "gb2", "xbf"], ["xbf"])
            for i in range(4):
                S.pe(lambda e, i=i, pb=pb: e.transpose(p_t[pb][:, i, :], xbf[:, i * 128:(i + 1) * 128], g.ident_f[:]), ["xbf"], [f"pt{pb}"])
            for i in range(2):
                S.pe(lambda e, i=i, pb=pb: e.transpose(p_u[pb][:, i, :], xbf[:, (4 + i) * 128:(5 + i) * 128], g.ident_f[:]), ["xbf"], [f"pu{pb}"])
            S.dve(lambda e, pb=pb, tb=tb: e.tensor_copy(out=qnT[:, :, tb * 128:(tb + 1) * 128], in_=p_t[pb][:]), [f"pt{pb}"], [])
            S.act(lambda e, pb=pb, tb=tb: e.copy(out=kvnT[:, :, tb * 128:(tb + 1) * 128], in_=p_u[pb][:]), [f"pu{pb}"], [])
        S.flush("mla_pre")


def _host_inputs(inp, b, shared):
    f = lambda a: np.ascontiguousarray(np.asarray(a, dtype=np.float32))
    m = dict(shared)
    m["xin"] = f(np.concatenate([np.asarray(inp["ctx"][b]), np.asarray(inp["x"][b])], axis=0))
    cc = np.stack([np.asarray(inp["c"][b]), np.asarray(inp["c_ctx"])], axis=0).reshape(2, KC, 128)
    m["cc"] = f(cc.transpose(2, 1, 0))
    return m


def _shared_inputs(inp):
    f = lambda a: np.ascontiguousarray(np.asarray(a, dtype=np.float32))
    m = {}
    m["ada_w"] = f(inp["ada_w"])
    m["ada_b"] = f(inp["ada_b"])
    m["ada_bcol"] = f(np.asarray(inp["ada_b"])[:, :4096].reshape(2, 32, 128).transpose(0, 2, 1))
    m["ln_g"] = f(inp["ln_g"])
    m["ln_b"] = f(inp["ln_b"])
    m["ab_w_in"] = f(inp["ab_w_in"][0])
    m["ab_b_if"] = f(inp["ab_b_if"][0])
    m["diff_lam"] = f(inp["diff_lam"][0])
    m["diff_norm_g"] = f(inp["diff_norm_g"])
    m["mlstm_norm_g"] = f(inp["mlstm_norm_g"])
    m["ab_w_out"] = f(inp["ab_w_out"][0])
    m["cd_w_in"] = f(inp["cd_w_in"][0])
    m["ret_decay"] = f(inp["ret_decay"][0])
    m["ret_norm_g"] = f(inp["ret_norm_g"])
    m["mla_q_norm_g"] = f(inp["mla_q_norm_g"])
    m["mla_w_uq"] = f(inp["mla_w_uq"][0])
    m["mla_kv_norm_g"] = f(inp["mla_kv_norm_g"])
    m["mla_w_ukv"] = f(inp["mla_w_ukv"][0])
    m["cd_w_out"] = f(inp["cd_w_out"][0])
    m["cmats"] = _const_mats()
    m["rope"] = _rope_tables()
    return m


def kernel(**inputs):
    nb = 8
    shared = _shared_inputs(inputs)
    in_maps = [_host_inputs(inputs, b, shared) for b in range(nb)]
    nc = build()
    res = run_bass_kernel_spmd(nc, in_maps, core_ids=list(range(nb)))
    return np.stack([np.asarray(r["out"], dtype=np.float32) for r in res.results], axis=0)


def scan_phase2(g, S, name, nheads, nj, gated, pfeat, qrow, krow, ptok, vcol, dst, dst_c0):
    nc = g.nc
    if MINI_HEADS is not None:
        nheads = MINI_HEADS
    aug = 257 if gated else 256
    NG = NTB if gated else 1
    with ExitStack() as es:
        sb = lambda nm, shape, dt=F32: es.enter_context(nc.sbuf_tensor(f"{name}_{nm}", shape, dt))
        ps_ = lambda nm, shape, dt=F32: es.enter_context(nc.psum_tensor(f"{name}_{nm}", shape, dt))
        x0 = [sb(f"x{j}", [128, NT]) for j in range(nj)]
        vt = sb("vt", [128, 9, 256])
        qTb = [sb(f"qTb{p}", [128, nj, NT], BF16) for p in range(2)]
        kTb = [sb(f"kTb{p}", [128, nj, NT], BF16) for p in range(2)]
        kTf = [sb(f"kTf{p}", [128, nj, NT]) for p in range(2)]
        NP = 2 if gated else 1
        vb = [sb(f"vb{p}", [128, NTB, aug], BF16) for p in range(2)]
        St = [[sb(f"St{p}{d}", [128, NTB, 128], BF16) for d in range(2)] for p in range(NP)]
        Kw = [[sb(f"Kw{p}{d}", [128, NTB, nj, 128], BF16) for d in range(2)] for p in range(NP)]
        if NP == 1:
            St = [St[0], St[0]]
            Kw = [Kw[0], Kw[0]]
        spk = (lambda p: p) if NP == 2 else (lambda p: 0)
        hacc = sb("hacc", [128, NTB, 256])
        Cst = [[sb(f"Cst{d}{j}", [128, aug]) for j in range(nj)] for d in range(2)]
        Cb = [[sb(f"Cb{d}{j}", [128, aug], BF16) for j in range(nj)] for d in range(2)]
        rcol = [sb(f"r{d}", [128, 1]) for d in range(2)]
        rneg = [sb(f"rn{d}", [128, 1]) for d in range(2)]
        Acol = sb("Acol", [128, NG, 16])
        Ucol = sb("Ucol", [128, NG, 16])
        Wcol = sb("Wcol", [128, NG, 16])
        ATb = sb("ATb", [128, NG, 16])
        LF = sb("LF", [128, NG, 16])
        p_g = [ps_(f"pg{i}", [128, 512]) for i in range(2)]
        p_t = [ps_(f"pt{i}", [128, nj, 128]) for i in range(2)]
        p_h = [ps_(f"ph{d}", [128, 512]) for d in range(2)]
        p_c = [ps_(f"pc{d}", [128, nj, 256 if nj == 2 else 512]) for d in range(2)]
        onec = g.ones_f[:, 0:1]
        if gated:
            G = sb("G", [128, NTB, 32])
            bif = sb("bif", [128, 32])
            GI = sb("GI", [128, NTB, 16])
            S.dma(G[:], ptok[:, 12288:12320].rearrange("(t p) n -> p t n", p=128), writes=["G"])
            S.dma(bif[:], g.ab_b_if.rearrange("a b -> (a b)").partition_broadcast(128), writes=["bif"])
            for tb in range(NTB):
                S.dve(lambda e, tb=tb: e.tensor_tensor(out=G[:, tb, :], in0=G[:, tb, :], in1=bif[:], op=ALU.add), ["G", "bif"], ["G"])
            S.dve(lambda e: e.tensor_copy(out=GI[:], in_=G[:, :, 0:16]), ["G"], ["GI"])
            S.act(lambda e: e.activation(out=LF[:], in_=G[:, :, 16:32], func=AF.Exp, scale=-1.0), ["G"], ["LF"])
            S.act(lambda e: e.activation(out=LF[:], in_=LF[:], func=AF.Ln, bias=onec), ["LF"], ["LF"])
            S.dve(lambda e: e.tensor_scalar(out=LF[:], in0=LF[:], scalar1=-1.0, scalar2=None, op0=ALU.mult), ["LF"], ["LF"])
            for p in range(2):
                S.pool(lambda e, p=p: e.memset(vb[p][:, :, 256:257], 1.0), [], [f"vb{p}"])
        else:
            cosT = sb("cos", [128, SEQ])
            sinT = sb("sin", [128, SEQ])
            t1 = sb("t1", [128, 1024])
            t2 = sb("t2", [128, 1024])
            GI = sb("GI", [128, 1, 16])
            S.dma(cosT[:], g.rope[2], writes=["cos"])
            S.dma(sinT[:], g.rope[3], writes=["sin"])
            S.dma(LF[:, 0, :], g.ret_decay.rearrange("a b -> (a b)").partition_broadcast(128), writes=["LF"])
            S.act(lambda e: e.activation(out=LF[:], in_=LF[:], func=AF.Exp, scale=-1.0), ["LF"], ["LF"])
            S.act(lambda e: e.activation(out=LF[:], in_=LF[:], func=AF.Ln, bias=onec), ["LF"], ["LF"])
            S.dve(lambda e: e.tensor_scalar(out=LF[:], in0=LF[:], scalar1=-1.0, scalar2=None, op0=ALU.mult), ["LF"], ["LF"])
            S.pool(lambda e: e.memset(GI[:], math.log(1.0 / 16.0)), [], ["GI"])
        LFc = [sb(f"LFc{d}", [128, NG, 8]) for d in range(2)]
        for d in range(2):
            M = g.maskU if d == 0 else g.maskL
            S.dve(lambda e, d=d: e.tensor_copy(out=LFc[d][:], in_=LF[:, :, d * 8:(d + 1) * 8]), ["LF"], [f"LFc{d}"])
            S.pe(lambda e, d=d, M=M: e.matmul(p_g[d][:, 0:NG * 8], lhsT=M[:], rhs=LFc[d][:].rearrange("p a b -> p (a b)"), start=True, stop=True),
                 [f"LFc{d}"], [f"pg{d}"])
            S.act(lambda e, d=d: e.activation(out=Acol[:, :, d * 8:(d + 1) * 8], in_=p_g[d][:, 0:NG * 8].rearrange("p (a b) -> p a b", b=8), func=AF.Exp),
                  [f"pg{d}"], [("Acol", d)])
            S.dve(lambda e, d=d: e.tensor_tensor(out=Ucol[:, :, d * 8:(d + 1) * 8], in0=GI[:, :, d * 8:(d + 1) * 8],
                                                 in1=p_g[d][:, 0:NG * 8].rearrange("p (a b) -> p a b", b=8), op=ALU.subtract), [f"pg{d}", "GI"], [("Ucol", d)])
            S.act(lambda e, d=d: e.activation(out=Ucol[:, :, d * 8:(d + 1) * 8], in_=Ucol[:, :, d * 8:(d + 1) * 8], func=AF.Exp), [("Ucol", d)], [("Ucol", d)])
        S.pe(lambda e: e.matmul(p_h[0][:, 0:NG * 16], lhsT=g.ones_f[:], rhs=LF[:].rearrange("p a b -> p (a b)"), start=True, stop=True), ["LF"], ["ph0"])
        S.act(lambda e: e.activation(out=ATb[:].rearrange("p a b -> p (a b)"), in_=p_h[0][:, 0:NG * 16], func=AF.Exp), ["ph0"], ["ATb"])
        S.dve(lambda e: e.tensor_tensor(out=Wcol[:], in0=Ucol[:], in1=ATb[:], op=ALU.mult), [("Ucol", 0), ("Ucol", 1), "ATb"], ["Wcol"])
        gk = [("Acol", 0), ("Acol", 1), ("Ucol", 0), ("Ucol", 1), "ATb", "Wcol"]
        gix = (lambda tb: tb) if gated else (lambda tb: 0)

        def load_head(h, p):
            for which, (rowf, dstT) in enumerate(((qrow, qTb[p]), (krow, kTb[p]))):
                nm = ("q" if which == 0 else "k") + str(p)
                for j in range(nj):
                    S.dma(x0[j][:], pfeat[rowf(h, j):rowf(h, j) + 128, :], writes=[f"x{j}"])
                if gated:
                    if which == 0:
                        S.dve(lambda e, dstT=dstT: e.tensor_scalar(out=dstT[:, 0, :], in0=x0[0][:], scalar1=128 ** -0.5, scalar2=None, op0=ALU.mult), ["x0"], [nm])
                    else:
                        S.dve(lambda e, dstT=dstT: e.tensor_copy(out=dstT[:, 0, :], in_=x0[0][:]), ["x0"], [nm])
                        S.act(lambda e: e.copy(out=kTf[p][:, 0, :], in_=x0[0][:]), ["x0"], [f"kf{p}"])
                else:
                    for hf in range(2):
                        lat = slice(CTX + hf * 1024, CTX + (hf + 1) * 1024)
                        tc_ = slice(hf * 1024, (hf + 1) * 1024)
                        S.dve(lambda e, lat=lat, tc_=tc_: e.tensor_tensor(out=t1[:], in0=x0[0][:, lat], in1=cosT[:, tc_], op=ALU.mult), ["x0", "cos"], ["t1"])
                        S.pool(lambda e, lat=lat, tc_=tc_: e.tensor_tensor(out=t2[:], in0=x0[1][:, lat], in1=sinT[:, tc_], op=ALU.mult), ["x1", "sin"], ["t2"])
                        S.dve(lambda e, lat=lat, dstT=dstT: e.tensor_tensor(out=dstT[:, 0, lat], in0=t1[:], in1=t2[:], op=ALU.subtract), ["t1", "t2"], [(nm, 0, hf)])
                        if which == 1:
                            S.pool(lambda e, lat=lat: e.tensor_tensor(out=kTf[p][:, 0, lat], in0=t1[:], in1=t2[:], op=ALU.subtract), ["t1", "t2"], [(f"kf{p}", 0, hf)])
                        S.dve(lambda e, lat=lat, tc_=tc_: e.tensor_tensor(out=t1[:], in0=x0[0][:, lat], in1=sinT[:, tc_], op=ALU.mult), ["x0", "sin"], ["t1"])
                        S.pool(lambda e, lat=lat, tc_=tc_: e.tensor_tensor(out=t2[:], in0=x0[1][:, lat], in1=cosT[:, tc_], op=ALU.mult), ["x1", "cos"], ["t2"])
                        S.dve(lambda e, lat=lat, dstT=dstT: e.tensor_tensor(out=dstT[:, 1, lat], in0=t1[:], in1=t2[:], op=ALU.add), ["t1", "t2"], [(nm, 1, hf)])
                        if which == 1:
                            S.pool(lambda e, lat=lat: e.tensor_tensor(out=kTf[p][:, 1, lat], in0=t1[:], in1=t2[:], op=ALU.add), ["t1", "t2"], [(f"kf{p}", 1, hf)])
                    for j in range(2):
                        S.act(lambda e, j=j, dstT=dstT: e.copy(out=dstT[:, j, 0:CTX], in_=x0[j][:, 0:CTX]), [f"x{j}"], [(nm, j, "c")])
                        if which == 1:
                            S.act(lambda e, j=j: e.copy(out=kTf[p][:, j, 0:CTX], in_=x0[j][:, 0:CTX]), [f"x{j}"], [(f"kf{p}", j, "c")])
            for hf in range(2):
                S.dma(vt[:], ptok[hf * 1152:(hf + 1) * 1152, vcol(h):vcol(h) + 256].rearrange("(t p) n -> p t n", p=128), writes=["vt"])
                S.pool(lambda e, hf=hf: e.tensor_copy(out=vb[p][:, hf * 9:(hf + 1) * 9, 0:256], in_=vt[:]), ["vt"], [(f"vb{p}", hf)])

        def qk_keys(p):
            if gated:
                return [f"q{p}"], [f"k{p}"]
            qs = [(f"q{p}", j, x) for j in range(2) for x in (0, 1, "c")]
            ks = [(f"k{p}", j, x) for j in range(2) for x in (0, 1, "c")]
            return qs, ks

        rot = [0]

        def stage1_item(h, p, tb):
            qs, ks = qk_keys(p)
            cols = slice(tb * 128, (tb + 1) * 128)
            r = rot[0] % 2
            rot[0] += 1
            for j in range(nj):
                S.pe(lambda e, j=j: e.matmul(p_g[r][:, 0:128], lhsT=kTb[p][:, j, cols], rhs=qTb[p][:, j, cols], start=(j == 0), stop=(j == nj - 1)),
                     qs + ks, [f"pg{r}"])
            kfk = [f"kf{p}"] if gated else [(f"kf{p}", j, x) for j in range(2) for x in (0, 1, "c")]
            for j in range(nj):
                S.pe(lambda e, j=j: e.transpose(p_t[r][:, j, :], kTf[p][:, j, cols], g.ident_f[:]), kfk, [f"pt{r}"])
            for d in range(2):
                M = g.maskU if d == 0 else g.maskL
                ci = d * 8 + h
                S.dve(lambda e, d=d, M=M, ci=ci: e.scalar_tensor_tensor(out=St[p][d][:, tb, :], in0=p_g[r][:, 0:128], scalar=Ucol[:, gix(tb), ci:ci + 1], in1=M[:],
                                                                         op0=ALU.mult, op1=ALU.mult), [f"pg{r}"] + gk, [(f"St{spk(p)}{d}", tb)])
                S.act(lambda e, d=d, ci=ci: e.activation(out=Kw[p][d][:, tb, :, :], in_=p_t[r][:], func=AF.Identity, scale=Wcol[:, gix(tb), ci:ci + 1]),
                      [f"pt{r}"] + gk, [(f"Kw{spk(p)}{d}", tb)])

        written = set()

        def stage2_step(h, p, d, tb):
            qs, ks = qk_keys(p)
            cols = slice(tb * 128, (tb + 1) * 128)
            ci = d * 8 + h
            vk = (f"vb{p}", tb // 9)
            for j in range(nj):
                S.pe(lambda e, j=j: e.matmul(p_h[d][:, 0:aug], lhsT=qTb[p][:, j, cols], rhs=Cb[d][j][:, 0:aug], start=(j == 0), stop=False),
                     qs + [f"Cb{d}{j}"], [f"ph{d}"])
            S.pe(lambda e: e.matmul(p_h[d][:, 0:aug], lhsT=St[p][d][:, tb, :], rhs=vb[p][:, tb, 0:aug], start=False, stop=True),
                 [(f"St{spk(p)}{d}", tb), vk, f"vb{p}"], [f"ph{d}"])
            hk = ("hacc", tb)
            acol = Acol[:, gix(tb), ci:ci + 1]
            if gated:
                S.dve(lambda e: e.tensor_scalar(out=rcol[d][:], in0=p_h[d][:, 256:257], scalar1=acol, scalar2=None, op0=ALU.mult), [f"ph{d}"] + gk, [f"r{d}"])
                S.dve(lambda e: e.tensor_scalar(out=rneg[d][:], in0=rcol[d][:], scalar1=-1.0, scalar2=None, op0=ALU.mult), [f"r{d}"], [f"rn{d}"])
                S.dve(lambda e: e.tensor_tensor(out=rcol[d][:], in0=rcol[d][:], in1=rneg[d][:], op=ALU.max), [f"r{d}", f"rn{d}"], [f"r{d}"])
                S.dve(lambda e: e.tensor_scalar(out=rcol[d][:], in0=rcol[d][:], scalar1=1.0, scalar2=None, op0=ALU.max), [f"r{d}"], [f"r{d}"])
                S.dve(lambda e: e.reciprocal(out=rcol[d][:], in_=rcol[d][:]), [f"r{d}"], [f"r{d}"])
                S.dve(lambda e: e.tensor_tensor(out=rcol[d][:], in0=rcol[d][:], in1=acol, op=ALU.mult), [f"r{d}"] + gk, [f"r{d}"])
                sc_ = rcol[d][:, 0:1]
                sk = [f"r{d}"]
            else:
                sc_ = acol
                sk = gk
            if (h, tb) not in written:
                S.dve(lambda e: e.tensor_scalar(out=hacc[:, tb, :], in0=p_h[d][:, 0:256], scalar1=sc_, scalar2=None, op0=ALU.mult), [f"ph{d}"] + sk, [hk])
            else:
                S.dve(lambda e: e.scalar_tensor_tensor(out=hacc[:, tb, :], in0=p_h[d][:, 0:256], scalar=sc_, in1=hacc[:, tb, :], op0=ALU.mult, op1=ALU.add),
                      [f"ph{d}", hk] + sk, [hk])
            written.add((h, tb))
            for j in range(nj):
                S.pe(lambda e, j=j: e.matmul(p_c[d][:, j, 0:aug], lhsT=Kw[p][d][:, tb, j, :], rhs=vb[p][:, tb, 0:aug], start=True, stop=True),
                     [(f"Kw{spk(p)}{d}", tb), vk, f"vb{p}"], [f"pc{d}"])
            for j in range(nj):
                S.dve(lambda e, j=j: e.scalar_tensor_tensor(out=Cst[d][j][:], in0=Cst[d][j][:], scalar=ATb[:, gix(tb), ci:ci + 1], in1=p_c[d][:, j, 0:aug],
                                                            op0=ALU.mult, op1=ALU.add), [f"Cst{d}{j}", f"pc{d}"] + gk, [f"Cst{d}{j}"])
                S.act(lambda e, j=j: e.copy(out=Cb[d][j][:], in_=Cst[d][j][:]), [f"Cst{d}{j}"], [f"Cb{d}{j}"])

        order = [list(range(NTB)), [1, 0] + list(range(NTB - 1, 1, -1))]
        load_head(0, 0)
        for tb in range(NTB):
            stage1_item(0, 0, tb)
        for h in range(nheads):
            p = h % 2
            if h + 1 < nheads:
                load_head(h + 1, (h + 1) % 2)
            for d in range(2):
                for j in range(nj):
                    S.pool(lambda e, d=d, j=j: e.memset(Cst[d][j][:], 0.0), [], [f"Cst{d}{j}"])
                    S.pool(lambda e, d=d, j=j: e.memset(Cb[d][j][:], 0.0), [], [f"Cb{d}{j}"])
            for i in range(NTB):
                for d in range(2):
                    stage2_step(h, p, d, order[d][i])
                if gated and h + 1 < nheads:
                    stage1_item(h + 1, (h + 1) % 2, i)
            if (not gated) and h + 1 < nheads:
                for i in range(NTB):
                    stage1_item(h + 1, (h + 1) % 2, i)
            c0 = dst_c0 + h * 256
            S.dma(dst[:, c0:c0 + 256].rearrange("(t p) n -> p t n", p=128), hacc[:], reads=[("hacc", tb) for tb in range(NTB)])
        S.flush(name)


def combine_phase(g, S, layer):
    nc = g.nc
    name = f"comb{layer}"
    ptok = g.ptok0 if layer == 0 else g.ptok1
    mix = g.mix0 if layer == 0 else g.mix1
    gate_c0 = 12320 if layer == 0 else 6976
    tbs = list(range(NTB)) if layer == 0 else list(range(2, NTB))
    if MINI_TBS is not None:
        tbs = tbs[:MINI_TBS]
    with ExitStack() as es:
        sb = lambda nm, shape, dt=F32: es.enter_context(nc.sbuf_tensor(f"{name}_{nm}", shape, dt))
        ps_ = lambda nm, shape, dt=F32: es.enter_context(nc.psum_tensor(f"{name}_{nm}", shape, dt))
        mr = [sb(f"mr{i}", [128, 4096]) for i in range(2)]
        gt = [sb(f"gt{i}", [128, 4096]) for i in range(2)]
        og = [sb(f"og{i}", [128, 2048]) for i in range(2)]
        tmp = sb("tmp", [128, 2048])
        gbc = sb("gbc", [128, 4096])
        mixb = [sb(f"mixb{i}", [128, 4096]) for i in range(2)]
        mixT = [sb(f"mixT{i}", [128, 32, 128], BF16) for i in range(2)]
        st = [sb(f"st{i}", [128, 16]) for i in range(4)]
        st2 = [sb(f"stb{i}", [128, 16]) for i in range(4)]
        p_t = [ps_(f"pt{i}", [128, 4, 128]) for i in range(4)]
        if layer == 0:
            S.dma(gbc[:, 0:2048], g.diff_norm_g.partition_broadcast(128), writes=["gbc"])
            S.dma(gbc[:, 2048:4096], g.mlstm_norm_g.partition_broadcast(128), writes=["gbc2"])
            S.dve(lambda e: e.tensor_scalar(out=gbc[:, 0:2048], in0=gbc[:, 0:2048], scalar1=0.8, scalar2=None, op0=ALU.mult), ["gbc"], ["gbc"])
        else:
            S.dma(gbc[:, 0:2048], g.ret_norm_g.partition_broadcast(128), writes=["gbc"])
        for it, tb in enumerate(tbs):
            b = it % 2
            rows = slice(tb * 128, (tb + 1) * 128)
            mrk, gtk, ogk, mbk = f"mr{b}", f"gt{b}", f"og{b}", f"mixb{b}"
            S.dma(mr[b][:], mix[rows, :], writes=[mrk])
            S.dma(gt[b][:], ptok[rows, gate_c0:gate_c0 + 4096], writes=[gtk])
            S.act(lambda e, b=b: e.activation(out=gt[b][:], in_=gt[b][:], func=AF.Silu), [gtk], [gtk])
            if layer == 0:
                S.dma(og[b][:], ptok[rows, 10240:12288], writes=[ogk])
                S.act(lambda e, b=b: e.activation(out=og[b][:], in_=og[b][:], func=AF.Sigmoid), [ogk], [ogk])
                group_norm(S, nc, mr[b][:, 0:2048].rearrange("p (g d) -> p g d", d=128), 16, 128, False,
                           tmp[:].rearrange("p (g d) -> p g d", d=128), [s_[:, 0:16] for s_ in st], mrk, "nA")
                group_norm(S, nc, mr[b][:, 2048:4096].rearrange("p (g d) -> p g d", d=256), 8, 256, True,
                           tmp[:].rearrange("p (g d) -> p g d", d=256), [s_[:, 0:8] for s_ in st2], mrk, "nB")
                S.pool(lambda e, b=b: e.tensor_tensor(out=gt[b][:, 2048:4096], in0=gt[b][:, 2048:4096], in1=og[b][:], op=ALU.mult), [gtk, ogk], [gtk])
                S.pool(lambda e, b=b: e.tensor_tensor(out=gt[b][:], in0=gt[b][:], in1=gbc[:], op=ALU.mult), [gtk, "gbc", "gbc2"], [gtk])
            else:
                group_norm(S, nc, mr[b][:, 0:2048].rearrange("p (g d) -> p g d", d=256), 8, 256, True,
                           tmp[:].rearrange("p (g d) -> p g d", d=256), [s_[:, 0:8] for s_ in st2], mrk, "nA")
                S.pool(lambda e, b=b: e.tensor_tensor(out=gt[b][:, 0:2048], in0=gt[b][:, 0:2048], in1=gbc[:, 0:2048], op=ALU.mult), [gtk, "gbc"], [gtk])
            S.dve(lambda e, b=b: e.tensor_tensor(out=mixb[b][:], in0=mr[b][:], in1=gt[b][:], op=ALU.mult), [mrk, gtk], [mbk])
            for q in range(8):
                pb = (it * 8 + q) % 4
                for i in range(4):
                    kc = q * 4 + i
                    S.pe(lambda e, kc=kc, pb=pb, i=i, b=b: e.transpose(p_t[pb][:, i, :], mixb[b][:, kc * 128:(kc + 1) * 128], g.ident_f[:]), [mbk], [f"pt{pb}"])
                if q % 2 == 0:
                    S.act(lambda e, b=b, q=q, pb=pb: e.copy(out=mixT[b][:, q * 4:(q + 1) * 4, :], in_=p_t[pb][:]), [f"pt{pb}"], [(f"mixT{b}", q)])
                else:
                    S.dve(lambda e, b=b, q=q, pb=pb: e.tensor_copy(out=mixT[b][:, q * 4:(q + 1) * 4, :], in_=p_t[pb][:]), [f"pt{pb}"], [(f"mixT{b}", q)])
            S.dma(g.mixT[layer][tb], mixT[b][:], reads=[(f"mixT{b}", q) for q in range(8)])
        S.flush(name)


def outproj_phase(g, S, layer):
    nc = g.nc
    name = f"oprj{layer}"
    W = (g.ab_w_out if layer == 0 else g.cd_w_out).rearrange("(k p) n -> p k n", p=128)
    tbs = list(range(NTB)) if layer == 0 else list(range(2, NTB))
    if MINI_TBS is not None:
        tbs = tbs[:MINI_TBS]
    with ExitStack() as es:
        wres = es.enter_context(nc.sbuf_tensor(f"{name}_wres", [128, 32, D], BF16))
        with ExitStack() as es1:
            wf = [es1.enter_context(nc.sbuf_tensor(f"{name}_wf{i}", [128, 8, 512], F32)) for i in range(2)]
            i = 0
            for cg in range(4):
                for k0 in range(0, 32, 8):
                    b = i % 2
                    i += 1
                    S.dma(wf[b][:], W[:, k0:k0 + 8, cg * 512:(cg + 1) * 512], writes=[f"wf{b}"])
                    dsto = wres[:, k0:k0 + 8, cg * 512:(cg + 1) * 512]
                    if i % 3 == 0:
                        S.pool(lambda e, b=b, dsto=dsto: e.tensor_copy(out=dsto, in_=wf[b][:]), [f"wf{b}"], [])
                    elif i % 3 == 1:
                        S.act(lambda e, b=b, dsto=dsto: e.copy(out=dsto, in_=wf[b][:]), [f"wf{b}"], [])
                    else:
                        S.dve(lambda e, b=b, dsto=dsto: e.tensor_copy(out=dsto, in_=wf[b][:]), [f"wf{b}"], [])
            S.flush(name + "w")
        sb = lambda nm, shape, dt=F32: es.enter_context(nc.sbuf_tensor(f"{name}_{nm}", shape, dt))
        ps_ = lambda nm, shape, dt=F32: es.enter_context(nc.psum_tensor(f"{name}_{nm}", shape, dt))
        mixT = [sb(f"mixT{i}", [128, 32, 128], BF16) for i in range(2)]
        xb = [sb(f"xb{i}", [128, D]) for i in range(2)]
        z = [sb(f"z{i}", [128, D]) for i in range(2)]
        grow = sb("grow", [128, D])
        lng = sb("lng", [128, D])
        lnb = sb("lnb", [128, D])
        ls = [sb(f"ls{i}", [128, 1]) for i in range(3)]
        p_y = [ps_(f"py{i}", [128, 512]) for i in range(8)]
        S.dma(lng[:], g.ln_g[layer:layer + 1, :].partition_broadcast(128), writes=["lng"])
        S.dma(lnb[:], g.ln_b[layer:layer + 1, :].partition_broadcast(128), writes=["lnb"])
        cur_w = None
        for it, tb in enumerate(tbs):
            b = it % 2
            rows = slice(tb * 128, (tb + 1) * 128)
            w = 1 if tb < 2 else 0
            if w != cur_w:
                S.dma(grow[:], g.gate_rows[layer, w:w + 1, :].partition_broadcast(128), writes=["grow"])
                cur_w = w
            S.dma(mixT[b][:], g.mixT[layer][tb], writes=[f"mixT{b}"])
            S.dma(xb[b][:], g.xsrc[layer][rows, :], writes=[f"xb{b}"])
            zk = [(f"z{b}", cg) for cg in range(4)]
            for cg in range(4):
                pb = (it % 2) * 4 + cg
                for kc in range(32):
                    S.pe(lambda e, b=b, kc=kc, cg=cg, pb=pb: e.matmul(p_y[pb][:], lhsT=mixT[b][:, kc, :], rhs=wres[:, kc, cg * 512:(cg + 1) * 512],
                                                                    start=(kc == 0), stop=(kc == 31)), [f"mixT{b}"], [f"py{pb}"])
                S.dve(lambda e, cg=cg, b=b, pb=pb: e.tensor_tensor(out=z[b][:, cg * 512:(cg + 1) * 512], in0=p_y[pb][:], in1=grow[:, cg * 512:(cg + 1) * 512], op=ALU.mult),
                      [f"py{pb}", "grow"], [(f"z{b}", cg)])
            S.dve(lambda e, b=b: e.scalar_tensor_tensor(out=z[b][:], in0=xb[b][:], scalar=float(ALPHA), in1=z[b][:], op0=ALU.mult, op1=ALU.add), zk + [f"xb{b}"], zk)
            S.dve(lambda e, b=b: e.reduce_sum(out=ls[0][:], in_=z[b][:], axis=AX.X), zk, ["ls0"])
            S.dve(lambda e: e.tensor_scalar(out=ls[0][:], in0=ls[0][:], scalar1=-1.0 / D, scalar2=None, op0=ALU.mult), ["ls0"], ["ls0"])
            S.act(lambda e, b=b: e.activation(out=z[b][:], in_=z[b][:], func=AF.Identity, bias=ls[0][:, 0:1]), zk + ["ls0"], zk)
            S.pool(lambda e, b=b: e.tensor_tensor(out=xb[b][:], in0=z[b][:], in1=z[b][:], op=ALU.mult), zk + [f"xb{b}"], [f"xb{b}"])
            S.dve(lambda e, b=b: e.reduce_sum(out=ls[1][:], in_=xb[b][:], axis=AX.X), [f"xb{b}"], ["ls1"])
            S.dve(lambda e: e.tensor_scalar(out=ls[1][:], in0=ls[1][:], scalar1=1.0 / D, scalar2=EPS, op0=ALU.mult, op1=ALU.add), ["ls1"], ["ls1"])
            S.act(lambda e: e.activation(out=ls[2][:], in_=ls[1][:], func=AF.Sqrt), ["ls1"], ["ls2"])
            S.dve(lambda e: e.reciprocal(out=ls[2][:], in_=ls[2][:]), ["ls2"], ["ls2"])
            S.dve(lambda e, b=b: e.scalar_tensor_tensor(out=z[b][:], in0=z[b][:], scalar=ls[2][:, 0:1], in1=lng[:], op0=ALU.mult, op1=ALU.mult), zk + ["ls2", "lng"], zk)
            S.pool(lambda e, b=b: e.tensor_tensor(out=z[b][:], in0=z[b][:], in1=lnb[:], op=ALU.add), zk + ["lnb"], zk)
            if layer == 0:
                S.dma(g.xs1[rows, :], z[b][:], reads=zk)
            else:
                S.dma(g.out[(tb - 2) * 128:(tb - 1) * 128, :], z[b][:], reads=zk)
        S.flush(name)
```
